# Optimizing a Trainium2 kernel written in Bass

```python
import math
import jax, jax.numpy as jnp
from jax import lax
import numpy as np

D_MODEL = 2048
BATCH = 4
SEQ = 4096
DEPTH = 1

CHUNK = 64
Q_BLOCK = 128
EPS = 1e-6

A_WIDTH = D_MODEL // 2
A_HEAD_DIM = 128
A_HEADS = A_WIDTH // A_HEAD_DIM

B_WIDTH = D_MODEL // 2
B_HEADS = 8
B_VDIM = B_WIDTH // B_HEADS
B_QKDIM = B_VDIM // 2

SPLIT_SIZES = [A_WIDTH, A_WIDTH, A_WIDTH, A_WIDTH,
               B_WIDTH, B_WIDTH, B_WIDTH, B_WIDTH,
               D_MODEL, D_MODEL]
N_IN = sum(SPLIT_SIZES)

kernel_name = "hgrn2_diffattn_gated_hybrid"


def rms_norm(x, gain):
    xf = x.astype(jnp.float32)
    y = xf * lax.rsqrt(jnp.mean(xf * xf, axis=-1, keepdims=True) + EPS)
    return (y * gain.astype(jnp.float32)).astype(x.dtype)


def hgrn2_mixer(q, f_logit, inp, lb, out_gain):
    Bsz, T, _ = q.shape
    n = T // CHUNK
    f32 = jnp.float32
    z = f_logit.astype(f32)
    lbf = lb.astype(f32)
    q = jax.nn.silu(q.astype(f32))
    log_f = jnp.log(lbf + (1.0 - lbf) * jax.nn.sigmoid(z))
    k = (1.0 - lbf) * jax.nn.sigmoid(-z)
    v = inp.astype(f32)

    def split(t):
        return t.reshape(Bsz, n, CHUNK, A_HEADS, A_HEAD_DIM).transpose(1, 0, 3, 2, 4)

    causal = jnp.tril(jnp.ones((CHUNK, CHUNK), dtype=bool))

    def step(S, xs):
        qc, kc, vc, lfc = xs
        b = jnp.cumsum(lfc, axis=2)
        diff = b[:, :, :, None, :] - b[:, :, None, :, :]
        decay = jnp.where(causal[None, None, :, :, None],
                          jnp.exp(jnp.minimum(diff, 0.0)), 0.0)
        scores = jnp.einsum('bhtd,bhsd,bhtsd->bhts', qc, kc, decay)
        intra = jnp.einsum('bhts,bhsv->bhtv', scores, vc)
        inter = jnp.einsum('bhtd,bhdv->bhtv', qc * jnp.exp(b), S)
        b_last = b[:, :, -1:, :]
        k_dec = kc * jnp.exp(b_last - b)
        S_new = S * jnp.exp(b_last[:, :, 0, :, None]) + jnp.einsum('bhsd,bhsv->bhdv', k_dec, vc)
        return S_new, intra + inter

    S0 = jnp.zeros((Bsz, A_HEADS, A_HEAD_DIM, A_HEAD_DIM), f32)
    _, out = lax.scan(step, S0, (split(q), split(k), split(v), split(log_f)))
    out = out.transpose(1, 0, 3, 2, 4).reshape(Bsz, T, A_HEADS, A_HEAD_DIM)
    out = rms_norm(out, out_gain)
    return out.reshape(Bsz, T, A_WIDTH)


def diff_attention(q, k, v, q_gain, k_gain, lam_vecs, subln_gain, lambda_init):
    Bsz, T, _ = q.shape
    f32 = jnp.float32
    q = rms_norm(q.reshape(Bsz, T, B_HEADS, 2, B_QKDIM), q_gain).astype(f32)
    k = rms_norm(k.reshape(Bsz, T, B_HEADS, 2, B_QKDIM), k_gain).astype(f32)
    q = q.transpose(0, 2, 1, 3, 4)
    k = k.transpose(0, 2, 1, 3, 4)
    v = v.astype(f32).reshape(Bsz, T, B_HEADS, B_VDIM).transpose(0, 2, 1, 3)
    lv = lam_vecs.astype(f32)
    lam = (jnp.exp(jnp.sum(lv[0] * lv[1])) - jnp.exp(jnp.sum(lv[2] * lv[3])) + lambda_init)
    slopes = jnp.exp2(-8.0 * (jnp.arange(B_HEADS, dtype=f32) + 1.0) / B_HEADS)
    scale = B_QKDIM ** -0.5
    outs = []
    for blk in range(T // Q_BLOCK):
        q0 = blk * Q_BLOCK
        kend = q0 + Q_BLOCK
        qb = q[:, :, q0:kend]
        kb = k[:, :, :kend]
        s = jnp.einsum('bhqcd,bhkcd->bhcqk', qb, kb) * scale
        tq = q0 + jnp.arange(Q_BLOCK)
        tk = jnp.arange(kend)
        dist = jnp.abs(tq[:, None] - tk[None, :]).astype(f32)
        allowed = (tk[None, :] // CHUNK) <= (tq[:, None] // CHUNK)
        s = s - (slopes[:, None, None] * dist)[None, :, None]
        s = jnp.where(allowed[None, None, None], s, -jnp.inf)
        p = jax.nn.softmax(s, axis=-1)
        w = p[:, :, 0] - lam * p[:, :, 1]
        outs.append(jnp.einsum('bhqk,bhkv->bhqv', w, v[:, :, :kend]))
    o = jnp.concatenate(outs, axis=2)
    o = rms_norm(o, subln_gain) * (1.0 - lambda_init)
    return o.transpose(0, 2, 1, 3).reshape(Bsz, T, B_WIDTH)


def setup_inputs(seed: int = 0) -> dict:
    key = jax.random.key(seed)
    ks = jax.random.split(key, 12)
    f32 = jnp.float32
    nrm = lambda k, s, sc: jax.random.normal(k, s, f32) * sc
    return {
        "x": nrm(ks[0], (BATCH, SEQ, D_MODEL), 1.0),
        "norm_w": 1.0 + nrm(ks[1], (DEPTH, D_MODEL), 0.02),
        "w_in": nrm(ks[2], (DEPTH, D_MODEL, N_IN), D_MODEL ** -0.5),
        "a_lower_bound": nrm(ks[3], (DEPTH + 1, A_WIDTH), 0.1),
        "a_out_norm": 1.0 + nrm(ks[4], (DEPTH, A_HEAD_DIM), 0.02),
        "b_q_norm": 1.0 + nrm(ks[5], (DEPTH, B_QKDIM), 0.02),
        "b_k_norm": 1.0 + nrm(ks[6], (DEPTH, B_QKDIM), 0.02),
        "b_lambda": nrm(ks[7], (DEPTH, 4, B_QKDIM), 0.1),
        "b_subln": 1.0 + nrm(ks[8], (DEPTH, B_VDIM), 0.02),
        "w_branch_a": nrm(ks[9], (DEPTH, A_WIDTH, D_MODEL), A_WIDTH ** -0.5),
        "w_branch_b": nrm(ks[10], (DEPTH, B_WIDTH, D_MODEL), B_WIDTH ** -0.5),
        "w_out": nrm(ks[11], (DEPTH, D_MODEL, D_MODEL), D_MODEL ** -0.5),
    }


def reference(x, norm_w, w_in, a_lower_bound, a_out_norm, b_q_norm, b_k_norm,
              b_lambda, b_subln, w_branch_a, w_branch_b, w_out):
    split_idx = [int(v) for v in np.cumsum(SPLIT_SIZES)[:-1]]
    lb_all = jnp.cumsum(jax.nn.softmax(a_lower_bound.astype(jnp.float32), axis=0), axis=0)
    for l in range(DEPTH):
        h = rms_norm(x, norm_w[l])
        proj = jnp.einsum('btd,dn->btn', h, w_in[l])
        (a_q, a_f, a_i, a_g, b_q, b_k, b_v, b_g,
         gate_a, gate_b) = jnp.split(proj, split_idx, axis=-1)
        ya = hgrn2_mixer(a_q, a_f, a_i, lb_all[l], a_out_norm[l]).astype(x.dtype) * jax.nn.silu(a_g)
        lambda_init = 0.8 - 0.6 * math.exp(-0.3 * l)
        yb = diff_attention(b_q, b_k, b_v, b_q_norm[l], b_k_norm[l], b_lambda[l],
                            b_subln[l], lambda_init).astype(x.dtype) * jax.nn.silu(b_g)
        ya = jnp.einsum('btw,wd->btd', ya, w_branch_a[l])
        yb = jnp.einsum('btw,wd->btd', yb, w_branch_b[l])
        mixed = jax.nn.sigmoid(gate_a) * ya + jax.nn.sigmoid(gate_b) * yb
        x = x + jnp.einsum('btd,de->bte', mixed, w_out[l])
    return x
```

```python
import numpy as np
import concourse.bass as bass
import concourse.mybir as mybir
from concourse.bass_utils import run_bass_kernel_spmd

F32 = mybir.dt.float32
BF16 = mybir.dt.bfloat16
AF = mybir.ActivationFunctionType
ALU = mybir.AluOpType
AX = mybir.AxisListType

D = 2048
TOK = 2048
NIN = 12288
EPS = 1e-6
NCORES = 8
LAMBDA_INIT = 0.2
NDMA_SEMS = 8

C_A = 0
C_BQ = 4096
C_BK = 5120
C_BV = 6144
C_BG = 7168
C_GA = 8192
C_GB = 10240


class Tok:
    __slots__ = ("sem", "val", "eng")

    def __init__(self, sem, val, eng):
        self.sem, self.val, self.eng = sem, val, eng


class Buf:
    __slots__ = ("name", "w", "r")

    def __init__(self, name=""):
        self.name, self.w, self.r = name, None, {}


class T:
    def __init__(self, ap, name=""):
        self.ap = ap
        self.buf = Buf(name)

    def __getitem__(self, k):
        return self.ap[k]


class Eng:
    def __init__(self, name, h, sem=None, dma_sems=None, seen=None):
        self.name, self.h, self.sem = name, h, sem
        self.count = 0
        self.dma_sems = dma_sems
        self.n = 0
        self.seen = {} if seen is None else seen
        self.pending = []

    def wait(self, tok):
        assert tok.val is not None, "waiting on an unresolved PE token"
        if self.seen.get(tok.sem, 0) >= tok.val:
            return
        self.h.wait_ge(tok.sem, tok.val)
        self.seen[tok.sem] = tok.val


class Kern:
    def __init__(self, nc):
        self.nc = nc
        self.stack = []
        mk = lambda n: self.enter(nc.semaphore(n))
        self.pe = Eng("pe", nc.tensor, mk("s_pe"))
        self.act = Eng("act", nc.scalar, mk("s_act"))
        self.dve = Eng("dve", nc.vector, mk("s_dve"))
        self.sp = Eng("sp", nc.sync, dma_sems=[mk(f"s_sp{i}") for i in range(NDMA_SEMS)])
        self.pq = Eng("pq", nc.gpsimd, dma_sems=[mk(f"s_pq{i}") for i in range(NDMA_SEMS)])
        self.aq = Eng("aq", nc.scalar, dma_sems=[mk(f"s_aq{i}") for i in range(NDMA_SEMS)], seen=self.act.seen)
        self.pool = Eng("pool", nc.gpsimd, mk("s_pool"), seen=self.pq.seen)
        self.engines = [self.pe, self.act, self.dve, self.sp, self.pq, self.aq, self.pool]

    def enter(self, cm):
        v = cm.__enter__()
        self.stack.append(cm)
        return v

    def close(self):
        while self.stack:
            self.stack.pop().__exit__(None, None, None)

    def op(self, eng, fn, reads=(), writes=(), inc=True):
        is_dma = eng.dma_sems is not None
        for b in reads:
            w = b.w
            if w is not None and not (w.eng is eng and eng.name == "pe"):
                eng.wait(w)
        for b in writes:
            w = b.w
            if w is not None and (is_dma or w.eng is not eng):
                eng.wait(w)
            for t in b.r.values():
                if is_dma or t.eng is not eng:
                    eng.wait(t)
        if is_dma:
            slot = eng.n % NDMA_SEMS
            rnd = eng.n // NDMA_SEMS
            sem = eng.dma_sems[slot]
            if rnd > 0 and eng.seen.get(sem, 0) < 16 * rnd:
                eng.h.wait_ge(sem, 16 * rnd)
                eng.seen[sem] = 16 * rnd
            ins = fn()
            ins.then_inc(sem, 16)
            tok = Tok(sem, 16 * (rnd + 1), eng)
            eng.n += 1
        else:
            ins = fn()
            if inc:
                eng.count += 1
                ins.then_inc(eng.sem, 1)
                tok = Tok(eng.sem, eng.count, eng)
                for p in eng.pending:
                    p.val = eng.count
                eng.pending = []
            else:
                tok = Tok(eng.sem, None, eng)
                eng.pending.append(tok)
        for b in reads:
            b.r[(tok.sem, id(eng))] = tok
        for b in writes:
            b.w = tok
            b.r = {}
        return tok

    def last_tokens(self):
        toks = []
        for e in self.engines:
            if e.dma_sems is None:
                assert not e.pending
                if e.count:
                    toks.append(Tok(e.sem, e.count, e))
            else:
                for i, sem in enumerate(e.dma_sems):
                    k = (e.n - i + NDMA_SEMS - 1) // NDMA_SEMS
                    if k > 0:
                        toks.append(Tok(sem, 16 * k, e))
        return toks

    def barrier(self):
        toks = self.last_tokens()
        for e in self.engines:
            for t in toks:
                if t.eng is e and e.dma_sems is None:
                    continue
                e.wait(t)

    def mm(self, out, lhsT, rhs, start, stop, reads, writes, inc):
        return self.op(self.pe, lambda: self.nc.tensor.matmul(out, lhsT, rhs, start=start, stop=stop),
                       reads, writes, inc)

    def tr(self, out, in_, ident, reads, writes, inc=True):
        return self.op(self.pe, lambda: self.nc.tensor.transpose(out, in_, ident), reads, writes, inc)

    def actf(self, out, in_, func, reads, writes, bias=0.0, scale=1.0, accum_out=None):
        def f():
            kw = {}
            if accum_out is not None:
                kw["accum_out"] = accum_out
            return self.nc.scalar.activation(out, in_, func, bias=bias, scale=scale, **kw)
        return self.op(self.act, f, reads, writes)

    def dma(self, eng, out, in_, reads, writes):
        return self.op(eng, lambda: eng.h.dma_start(out=out, in_=in_), reads, writes)


def _consts():
    i = np.arange(128)[:, None]
    j = np.arange(128)[None, :]
    trip = (i <= j).astype(np.float32) - (i <= 63).astype(np.float32)
    triu = (i <= j).astype(np.float32)
    bd = ((i // 64) == (j // 64)).astype(np.float32) / 64.0
    o128 = np.full((128, 128), 1.0 / 128.0, np.float32)
    sel = np.concatenate([(i <= 63), (i >= 64)], axis=1).astype(np.float32)
    cf = np.concatenate([trip, triu, bd, o128, sel, np.ones((128, 128), np.float32)], axis=1).astype(np.float32)
    cb = np.concatenate([np.eye(128, dtype=np.float32), np.ones((128, 128), np.float32), trip, sel], axis=1)
    tk = np.arange(4096)
    kpos = np.stack([64.0 * (tk // 64), (tk % 64).astype(np.float64), np.ones(4096), np.ones(4096)]).astype(np.float32)
    tq = TOK + np.arange(TOK)
    qpos = np.zeros((8, 4, TOK), np.float32)
    for h in range(8):
        sl = 2.0 ** (-(h + 1))
        qpos[h] = np.stack([np.full(TOK, sl), np.full(TOK, sl), -sl * 64.0 * (tq // 64), -sl * (tq % 64)])
    jq = np.arange(512)[None, :]
    dg = []
    for bi in range(4):
        kp = 128 * bi + i
        allowed = (kp // 64) <= (jq // 64)
        fix = np.where(kp > jq, -2.0 * (kp - jq), 0.0)
        dg.append(np.where(allowed, fix, -1.0e9).astype(np.float32))
    dg = np.concatenate(dg, axis=1)
    return cf, cb, kpos, qpos, dg


def build(debug=False):
    nc = bass.Bass("TRN2", target_bir_lowering=False)
    K = Kern(nc)
    pe, act, dve, sp, pq = K.pe, K.act, K.dve, K.sp, K.pq
    V = nc.vector
    dt_in = lambda n, s: nc.dram_tensor(n, s, F32, kind="ExternalInput").ap()
    xo = dt_in("xo", [TOK, D])
    xp = dt_in("xp", [TOK, D])
    w_in = dt_in("w_in", [D, NIN])
    wa = dt_in("wa", [1024, D])
    wb = dt_in("wb", [1024, D])
    wo = dt_in("wo", [D, D])
    norm_w = dt_in("norm_w", [D])
    alb = dt_in("alb", [2, 1024])
    aon = dt_in("aon", [128])
    bqn = dt_in("bqn", [64])
    bkn = dt_in("bkn", [64])
    blam = dt_in("blam", [256])
    bsub = dt_in("bsub", [128])
    c_f = dt_in("c_f", [128, 642])
    c_b = dt_in("c_b", [128, 386])
    c_kpos = dt_in("c_kpos", [4, 4096])
    c_qpos = dt_in("c_qpos", [32, TOK])
    c_dg = dt_in("c_dg", [128, 2048])
    c_mask = dt_in("c_mask", [128, 1])
    y = nc.dram_tensor("y", [TOK, D], F32, kind="ExternalOutput").ap()
    scr = lambda n, s: T(nc.dram_tensor(n, s, BF16, kind="Internal").ap(), n)
    kc = [scr(f"kc{h}", [128, 2 * TOK]) for h in range(8)]
    vc = scr("vc", [2 * TOK, 1024])
    qb = [scr(f"qb{h}", [128, TOK]) for h in range(8)]
    gb = [scr(f"gb{h}", [128, TOK]) for h in range(8)]
    gs = [scr(f"gs{j}", [128, TOK]) for j in range(32)]
    yas = [scr(f"yas{h}", [128, TOK]) for h in range(8)]
    dbg = {}

    def sb(name, shape, dt=F32):
        return T(K.enter(nc.sbuf_tensor(name, shape, dt)), name)

    PS = [T(K.enter(nc.psum_tensor(f"ps{i}", [128, 512], F32)), f"ps{i}") for i in range(8)]

    cf = sb("cf", [128, 642])
    cb = sb("cb", [128, 386], BF16)
    cmask = sb("cmask", [128, 1])
    K.dma(sp, cf[:, :], c_f[:, :], [], [cf.buf])
    K.dma(pq, cb[:, :], c_b[:, :], [], [cb.buf])
    K.dma(sp, cmask[:, :], c_mask[:, :], [], [cmask.buf])
    TRIP, TRIU, BD, O128, SEL = (cf[:, 0:128], cf[:, 128:256], cf[:, 256:384], cf[:, 384:512], cf[:, 512:514])
    ONESF = cf[:, 514:642]
    IDENT, ONESB, TRIPB, SELB = cb[:, 0:128], cb[:, 128:256], cb[:, 256:384], cb[:, 384:386]

    gainA = sb("gainA", [128, 128])
    K.dma(sp, gainA[:, :], aon.partition_broadcast(128), [], [gainA.buf])
    pv = sb("pv", [128, 16])
    pvr = sb("pvr", [128, 4])
    with nc.allow_non_contiguous_dma(reason="tiny parameter vectors"):
        for half in range(2):
            K.dma(sp, pvr[64 * half:64 * half + 64, 0:1], bqn.rearrange("(p o) -> p o", o=1), [], [pvr.buf])
            K.dma(sp, pvr[64 * half:64 * half + 64, 1:2], bkn.rearrange("(p o) -> p o", o=1), [], [pvr.buf])
        K.dma(sp, pvr[:, 2:3], bsub.rearrange("(p o) -> p o", o=1), [], [pvr.buf])
    lam4 = sb("lam4", [128, 256])
    K.dma(sp, lam4[:, :], blam.partition_broadcast(128), [], [lam4.buf])
    lamt = sb("lamt", [128, 128])
    K.op(dve, lambda: V.tensor_scalar(pv[:, 0:1], pvr[:, 0:1], 0.125, None, ALU.mult), [pvr.buf], [pv.buf])
    K.op(dve, lambda: V.tensor_copy(pv[:, 1:2], pvr[:, 1:2]), [pvr.buf], [pv.buf])
    K.op(dve, lambda: V.tensor_scalar(pv[:, 2:3], pvr[:, 2:3], 1.0 - LAMBDA_INIT, None, ALU.mult), [pvr.buf], [pv.buf])
    K.op(dve, lambda: V.tensor_tensor(lamt[:, 0:64], lam4[:, 0:64], lam4[:, 64:128], ALU.mult), [lam4.buf], [lamt.buf])
    K.op(dve, lambda: V.tensor_tensor(lamt[:, 64:128], lam4[:, 128:192], lam4[:, 192:256], ALU.mult), [lam4.buf], [lamt.buf])
    K.op(dve, lambda: V.reduce_sum(pv[:, 4:5], lamt[:, 0:64], axis=AX.X), [lamt.buf], [pv.buf])
    K.op(dve, lambda: V.reduce_sum(pv[:, 5:6], lamt[:, 64:128], axis=AX.X), [lamt.buf], [pv.buf])
    K.actf(pv[:, 6:8], pv[:, 4:6], AF.Exp, [pv.buf], [pv.buf])
    K.op(dve, lambda: V.tensor_tensor(pv[:, 8:9], pv[:, 7:8], pv[:, 6:7], ALU.subtract), [pv.buf], [pv.buf])
    K.op(dve, lambda: V.tensor_scalar(pv[:, 3:4], pv[:, 8:9], -LAMBDA_INIT, None, ALU.add), [pv.buf], [pv.buf])
    GQ, GK, SUBLN, NEGLAM = pv[:, 0:1], pv[:, 1:2], pv[:, 2:3], pv[:, 3:4]


    n_proj = len(K.stack)
    hT = sb("hT", [128, 16, TOK], BF16)
    hbuf = [[Buf(f"h{t}_{hf}") for hf in range(2)] for t in range(16)]
    nwB = sb("nwB", [128, D])
    K.dma(sp, nwB[:, :], norm_w.partition_broadcast(128), [], [nwB.buf])
    omlB = sb("omlB", [128, 1024])
    a01 = sb("a01", [128, 2, 1024])
    K.dma(sp, a01[:, 0, :], alb[0, :].partition_broadcast(128), [], [a01.buf])
    K.dma(sp, a01[:, 1, :], alb[1, :].partition_broadcast(128), [], [a01.buf])
    K.op(dve, lambda: V.tensor_tensor(a01[:, 0, :], a01[:, 1, :], a01[:, 0, :], ALU.subtract), [a01.buf], [a01.buf])
    K.actf(omlB[:, :], a01[:, 0, :], AF.Sigmoid, [a01.buf], [omlB.buf])
    K.op(dve, lambda: V.tensor_scalar(omlB[:, :], omlB[:, :], -0.5, None, ALU.mult), [omlB.buf], [omlB.buf])

    Wt = [sb(f"Wt{i}", [128, 16, 512], BF16) for i in range(2)]
    wcnt = [0]
    xt = [sb(f"xt{i}", [128, D]) for i in range(2)]
    xn = sb("xn", [128, D], BF16)
    junk = sb("junk", [128, D], BF16)
    st1 = sb("st1", [128, 8])
    Sst = [sb(f"S{h}", [128, 128]) for h in range(8)]
    stg = [sb(f"stg{i}", [128, 512], BF16) for i in range(4)]
    stgc = [0]
    nsq = [sb(f"nsq{i}", [128, 512]) for i in range(2)]
    nrs = [sb(f"nrs{i}", [128, 512]) for i in range(2)]
    ncnt = [0]
    raw = [sb(f"raw{i}", [128, 4, 512]) for i in range(2)]
    vbq = [sb(f"vbq{i}", [128, 4, 128], BF16) for i in range(2)]
    hq = {n: sb("hq_" + n, shp, dt) for n, shp, dt in [
        ("lf", [128, 4, 128], F32), ("lfh", [128, 4, 128], BF16), ("lfl", [128, 4, 128], BF16), ("ep", [128, 4, 128], F32), ("en", [128, 4, 128], F32),
        ("qt", [128, 4, 128], BF16), ("kt", [128, 4, 128], BF16), ("qkT", [128, 1024], BF16),
        ("pm", [128, 4, 128], BF16), ("ug", [128, 4, 128], F32), ("sa", [128, 5, 128], F32),
        ("sp", [128, 4, 128], BF16), ("osq", [128, 4, 128], F32), ("ya", [128, 4, 128], BF16),
        ("ebc", [128, 4, 2], F32), ("gc", [128, 4], F32), ("ss", [128, 4], F32), ("lnv", [128, 4], F32),
        ("rstd", [128, 4], F32), ("yat", [128, 512], BF16), ("gg", [128, 4, 128], F32)]}

    def build_front(t):
        if True:
            x_t = xt[t % 2]
            K.actf(junk[:, :], x_t[:, :], AF.Square, [x_t.buf], [junk.buf, st1.buf], accum_out=st1[:, 0:1])
            K.actf(st1[:, 1:2], st1[:, 0:1], AF.Sqrt, [st1.buf], [st1.buf], bias=EPS, scale=1.0 / D)
            K.op(dve, lambda: V.reciprocal(st1[:, 2:3], st1[:, 1:2]), [st1.buf], [st1.buf])
            K.op(dve, lambda: V.scalar_tensor_tensor(xn[:, :], x_t[:, :], st1[:, 2:3], nwB[:, :], ALU.mult, ALU.mult),
                 [x_t.buf, st1.buf, nwB.buf], [xn.buf])

    def build_back(t):
        if True:
            for half in range(2):
                bank = PS[half]
                pb = bank.ap[:, :].bitcast(BF16)
                for c in range(8):
                    kcix = half * 8 + c
                    K.tr(pb[:, c * 128:(c + 1) * 128], xn[:, kcix * 128:(kcix + 1) * 128], IDENT,
                         [xn.buf, cb.buf], [bank.buf], inc=(c == 7))
                src = pb.rearrange("p (a b) -> p a b", b=128)
                dst = hT[:, half * 8:half * 8 + 8, t * 128:(t + 1) * 128]
                if half == 0:
                    K.op(dve, lambda: V.tensor_copy(dst, src), [bank.buf], [hbuf[t][0]])
                else:
                    K.op(act, lambda: nc.scalar.copy(dst, src), [bank.buf], [hbuf[t][1]])

    def load_w(c0, ncols):
        w = Wt[wcnt[0] % 2]
        wcnt[0] += 1
        src = w_in[:, c0:c0 + ncols].rearrange("(kc p) n -> p kc n", p=128)
        K.dma(pq, w[:, :, 0:ncols], src, [], [w.buf])
        return w

    def gemm_fm(w, col, n, bank):
        for k in range(16):
            K.mm(bank.ap[:, :], w[:, k, col:col + 128], hT[:, k, n * 512:(n + 1) * 512], k == 0, k == 15,
                 [w.buf] + sum(hbuf[4 * n:4 * n + 4], []), [bank.buf], inc=(k == 15))

    def gemm_tm(w, col, ncols, t, bank):
        for k in range(16):
            K.mm(bank.ap[:, 0:ncols], hT[:, k, t * 128:(t + 1) * 128], w[:, k, col:col + ncols], k == 0, k == 15,
                 [w.buf] + hbuf[t], [bank.buf], inc=(k == 15))

    def next_stg():
        s = stg[stgc[0] % 4]
        stgc[0] += 1
        return s

    def qknorm(bank, gain, dst_dram, dst_ap):
        i = ncnt[0] % 2
        ncnt[0] += 1
        sq, rs = nsq[i], nrs[i]
        K.actf(sq[:, :], bank.ap[:, :], AF.Square, [bank.buf], [sq.buf])
        msb = PS[4]
        K.mm(msb.ap[:, :], BD, sq[:, :], True, True, [cf.buf, sq.buf], [msb.buf], inc=True)
        K.actf(rs[:, :], msb.ap[:, :], AF.Sqrt, [msb.buf], [rs.buf], bias=EPS)
        K.op(dve, lambda: V.reciprocal(rs[:, :], rs[:, :]), [rs.buf], [rs.buf])
        s = next_stg()
        K.op(dve, lambda: V.scalar_tensor_tensor(s[:, :], bank.ap[:, :], gain, rs[:, :], ALU.mult, ALU.mult),
             [bank.buf, rs.buf, pv.buf], [s.buf])
        K.dma(sp, dst_ap, s[:, :], [s.buf], [dst_dram.buf])

    gcnt = [0]

    def next_bank():
        b = PS[2 + gcnt[0] % 2]
        gcnt[0] += 1
        return b

    def proj_k(tok_off):
        for g in range(2):
            w = load_w(C_BK + 512 * g, 512)
            for m in range(4):
                h = 4 * g + m
                for n in range(4):
                    bank = next_bank()
                    gemm_fm(w, 128 * m, n, bank)
                    qknorm(bank, GK, kc[h], kc[h][:, tok_off + n * 512: tok_off + (n + 1) * 512])

    def proj_q():
        for g in range(2):
            w = load_w(C_BQ + 512 * g, 512)
            for m in range(4):
                h = 4 * g + m
                for n in range(4):
                    bank = next_bank()
                    gemm_fm(w, 128 * m, n, bank)
                    qknorm(bank, GQ, qb[h], qb[h][:, n * 512:(n + 1) * 512])

    def proj_act_fm(c0, func, dsts):
        ngroups = len(dsts) // 4
        for g in range(ngroups):
            w = load_w(c0 + 512 * g, 512)
            for m in range(4):
                for n in range(4):
                    bank = next_bank()
                    gemm_fm(w, 128 * m, n, bank)
                    s = next_stg()
                    K.actf(s[:, :], bank.ap[:, :], func, [bank.buf], [s.buf])
                    d = dsts[4 * g + m]
                    K.dma(sp, d[:, n * 512:(n + 1) * 512], s[:, :], [s.buf], [d.buf])

    def build_and_v(xsrc, tok_off):
        ws = [load_w(C_BV + 512 * g, 512) for g in range(2)]
        def load_x(t):
            K.dma(sp, xt[t % 2][:, :], xsrc[t * 128:(t + 1) * 128, :], [], [xt[t % 2].buf])
        load_x(0)
        load_x(1)
        build_front(0)
        build_back(0)
        for t in range(16):
            if t + 1 < 16:
                build_front(t + 1)
                if t + 2 < 16:
                    load_x(t + 2)
            for g in range(2):
                w = ws[g]
                bank = next_bank()
                gemm_tm(w, 0, 512, t, bank)
                s = next_stg()
                K.op(dve, lambda: V.tensor_copy(s[:, :], bank.ap[:, :]), [bank.buf], [s.buf])
                K.dma(sp, vc[tok_off + t * 128: tok_off + (t + 1) * 128, g * 512:(g + 1) * 512], s[:, :],
                      [s.buf], [vc.buf])
            if t + 1 < 16:
                build_back(t + 1)

    GB = [PS[0], PS[1], PS[2], PS[3]]
    X1, X2, X3, X4 = PS[4], PS[5], PS[6], PS[7]

    def bc4(ap2d):
        return ap2d.unsqueeze(1).broadcast_to([128, 4, 128])

    def pass1(P, vb, own):
        zc, ic = (1, 2) if own else (0, 1)
        z = P[:, :, zc * 128:(zc + 1) * 128]
        K.actf(z, z, AF.Tanh, [P.buf], [P.buf], scale=0.5)
        if own:
            q = P[:, :, 0:128]
            gg = P[:, :, 384:512]
            K.actf(q, q, AF.Silu, [P.buf], [P.buf])
            K.actf(gg, gg, AF.Silu, [P.buf], [P.buf])
        K.actf(vb[:, :, :], P[:, :, ic * 128:(ic + 1) * 128], AF.Copy, [P.buf], [vb.buf])

    def pass2(h, qd, P, vb, own):
        g = hq
        S = Sst[h]
        zc = 1 if own else 0
        kk = P[:, :, zc * 128:(zc + 1) * 128]
        K.op(dve, lambda: V.scalar_tensor_tensor(kk, kk, 1.0, bc4(omlB[:, h * 128:(h + 1) * 128]), ALU.subtract, ALU.mult),
             [P.buf, omlB.buf], [P.buf])
        K.actf(g["lf"][:, :, :], kk, AF.Ln, [P.buf], [g["lf"].buf], bias=1.0, scale=-1.0)
        K.op(dve, lambda: V.tensor_copy(g["lfh"][:, :, :], g["lf"][:, :, :]), [g["lf"].buf], [g["lfh"].buf])
        K.op(dve, lambda: V.tensor_tensor(g["lfl"][:, :, :], g["lf"][:, :, :], g["lfh"][:, :, :], ALU.subtract),
             [g["lf"].buf, g["lfh"].buf], [g["lfl"].buf])
        yield
        for t in range(4):
            K.mm(X1.ap[:, t * 128:(t + 1) * 128], TRIPB, g["lfh"][:, t, :], True, False, [cb.buf, g["lfh"].buf], [X1.buf], inc=False)
            K.mm(X1.ap[:, t * 128:(t + 1) * 128], TRIPB, g["lfl"][:, t, :], False, True, [cb.buf, g["lfl"].buf], [X1.buf], inc=(t == 3))
        for t in range(4):
            K.mm(X2.ap[:, 2 * t:2 * t + 2], g["lfh"][:, t, :], SELB, True, False, [cb.buf, g["lfh"].buf], [X2.buf], inc=False)
            K.mm(X2.ap[:, 2 * t:2 * t + 2], g["lfl"][:, t, :], SELB, False, True, [cb.buf, g["lfl"].buf], [X2.buf], inc=(t == 3))
        yield
        x1v = X1.ap[:, :].rearrange("p (a b) -> p a b", b=128)
        K.actf(g["en"][:, :, :], x1v, AF.Exp, [X1.buf], [g["en"].buf], scale=-1.0)
        K.actf(g["ebc"][:, :, :], X2.ap[:, 0:8].rearrange("p (a b) -> p a b", b=2), AF.Exp, [X2.buf], [g["ebc"].buf])
        if own:
            K.actf(g["ep"][:, :, :], x1v, AF.Exp, [X1.buf], [g["ep"].buf])
        K.op(dve, lambda: V.tensor_tensor(g["kt"][:, :, :], kk, g["en"][:, :, :], ALU.mult), [P.buf, g["en"].buf], [g["kt"].buf])
        if own:
            K.op(dve, lambda: V.tensor_tensor(g["qt"][:, :, :], P[:, :, 0:128], g["ep"][:, :, :], ALU.mult),
                 [P.buf, g["ep"].buf], [g["qt"].buf])
            K.op(dve, lambda: V.tensor_tensor(g["gg"][:, :, :], P[:, :, 384:512], bc4(gainA[:, :]), ALU.mult),
                 [P.buf, gainA.buf], [g["gg"].buf])
        K.op(dve, lambda: V.tensor_tensor(g["gc"][:, 0:3], g["ebc"][:, 1:4, 0], g["ebc"][:, 0:3, 1], ALU.mult),
             [g["ebc"].buf], [g["gc"].buf])
        K.op(dve, lambda: V.tensor_copy(g["gc"][:, 3:4], g["ebc"][:, 3, 1:2]), [g["ebc"].buf], [g["gc"].buf])
        K.op(dve, lambda: V.tensor_scalar(g["sa"][:, 0, :], S[:, :], g["ebc"][:, 0, 0:1], None, ALU.mult),
             [S.buf, g["ebc"].buf], [g["sa"].buf])
        yield
        for t in range(4):
            K.mm(X3.ap[:, t * 128:(t + 1) * 128], g["kt"][:, t, :], vb[:, t, :], True, True, [g["kt"].buf, vb.buf], [X3.buf], inc=(t == 3))
        if own:
            x2b = X2.ap[:, :].bitcast(BF16)
            for t in range(4):
                K.tr(x2b[:, t * 128:(t + 1) * 128], g["qt"][:, t, :], IDENT, [g["qt"].buf, cb.buf], [X2.buf], inc=False)
            for t in range(4):
                K.tr(x2b[:, 512 + t * 128:512 + (t + 1) * 128], g["kt"][:, t, :], IDENT, [g["kt"].buf, cb.buf], [X2.buf], inc=(t == 3))
        yield
        K.op(dve, lambda: V.tensor_tensor(g["ug"][:, :, :], X3.ap[:, :].rearrange("p (a b) -> p a b", b=128),
                                          g["gc"][:, 0:4].unsqueeze(2).broadcast_to([128, 4, 128]), ALU.mult),
             [X3.buf, g["gc"].buf], [g["ug"].buf])
        if own:
            K.actf(g["qkT"][:, :], x2b[:, :], AF.Copy, [X2.buf], [g["qkT"].buf])
        yield
        for t in range(4):
            dst = g["sa"][:, t + 1, :] if t < 3 else S[:, :]
            dbuf = g["sa"].buf if t < 3 else S.buf
            K.op(dve, lambda: V.scalar_tensor_tensor(dst, g["sa"][:, t, :], g["gc"][:, t:t + 1], g["ug"][:, t, :], ALU.mult, ALU.add),
                 [g["sa"].buf, g["gc"].buf, g["ug"].buf], [dbuf])
        if not own:
            return
        K.actf(g["sp"][:, :, :], g["sa"][:, 0:4, :], AF.Copy, [g["sa"].buf], [g["sp"].buf])
        for t in range(4):
            K.mm(X1.ap[:, t * 128:(t + 1) * 128], g["qkT"][:, 512 + t * 128:512 + (t + 1) * 128], g["qkT"][:, t * 128:(t + 1) * 128],
                 True, True, [g["qkT"].buf], [X1.buf], inc=(t == 3))
        yield
        K.op(dve, lambda: V.tensor_tensor(g["pm"][:, :, :], x1v, bc4(TRIU), ALU.mult), [X1.buf, cf.buf], [g["pm"].buf])
        yield
        for t in range(4):
            K.mm(X4.ap[:, t * 128:(t + 1) * 128], g["pm"][:, t, :], vb[:, t, :], True, False, [g["pm"].buf, vb.buf], [X4.buf], inc=False)
            K.mm(X4.ap[:, t * 128:(t + 1) * 128], g["qkT"][:, t * 128:(t + 1) * 128], g["sp"][:, t, :], False, True,
                 [g["qkT"].buf, g["sp"].buf], [X4.buf], inc=(t == 3))
        yield
        x4v = X4.ap[:, :].rearrange("p (a b) -> p a b", b=128)
        K.actf(g["osq"][:, :, :], x4v, AF.Square, [X4.buf], [g["osq"].buf])
        K.op(dve, lambda: V.reduce_sum(g["ss"][:, :], g["osq"][:, :, :], axis=AX.X), [g["osq"].buf], [g["ss"].buf])
        K.actf(g["lnv"][:, :], g["ss"][:, :], AF.Ln, [g["ss"].buf], [g["lnv"].buf], bias=EPS, scale=1.0 / 128.0)
        K.actf(g["rstd"][:, :], g["lnv"][:, :], AF.Exp, [g["lnv"].buf], [g["rstd"].buf], scale=-0.5)
        for t in range(4):
            K.op(dve, lambda: V.scalar_tensor_tensor(g["ya"][:, t, :], X4.ap[:, t * 128:(t + 1) * 128], g["rstd"][:, t:t + 1],
                                                     g["gg"][:, t, :], ALU.mult, ALU.mult),
                 [X4.buf, g["rstd"].buf, g["gg"].buf], [g["ya"].buf])
        yield
        x3b = X3.ap[:, :].bitcast(BF16)
        for t in range(4):
            K.tr(x3b[:, t * 128:(t + 1) * 128], g["ya"][:, t, :], IDENT, [g["ya"].buf, cb.buf], [X3.buf], inc=(t == 3))
        yield
        K.actf(g["yat"][:, :], x3b[:, 0:512], AF.Copy, [X3.buf], [g["yat"].buf])
        K.dma(sp, yas[h][:, qd * 512:(qd + 1) * 512], g["yat"][:, :], [g["yat"].buf], [yas[h].buf])

    def proj_a(own):
        ncols = 512 if own else 256
        sched = [1, 2, 2, 2] if own else [1, 2, 2, 1]
        tsched = [2, 0, 0, 0]
        cur, tail = None, None
        qcount = 0

        def adv(gen, k):
            if gen is None:
                return None
            for _ in range(k):
                if next(gen, "done") == "done":
                    return None
            return gen

        for h in range(8):
            w = load_w(C_A + 512 * h + (0 if own else 128), ncols)
            for qd in range(4):
                P, vb = raw[qcount % 2], vbq[qcount % 2]
                qcount += 1
                for tt in range(4):
                    t = 4 * qd + tt
                    bank = GB[tt]
                    gemm_tm(w, 0, ncols, t, bank)
                    K.op(dve, lambda: V.tensor_copy(P[:, tt, 0:ncols], bank.ap[:, 0:ncols]), [bank.buf], [P.buf])
                    tail = adv(tail, tsched[tt])
                    if tt == 1 and tail is not None:
                        for _ in tail:
                            pass
                        tail = None
                    cur = adv(cur, sched[tt])
                if own:
                    cur = adv(cur, 2)
                tail = cur
                pass1(P, vb, own)
                cur = pass2(h, qd, P, vb, own)
        for gen in (tail, cur):
            if gen is not None:
                for _ in gen:
                    pass

    for h in range(8):
        K.op(dve, lambda: V.memset(Sst[h][:, :], 0.0), [], [Sst[h].buf])

    build_and_v(xp, 0)
    proj_k(0)
    proj_a(False)
    build_and_v(xo, TOK)
    proj_k(TOK)
    proj_q()
    proj_act_fm(C_BG, AF.Silu, gb)
    proj_act_fm(C_GA, AF.Sigmoid, gs)
    proj_a(True)


    K.barrier()
    while len(K.stack) > n_proj:
        K.stack.pop().__exit__(None, None, None)

    ybT = [sb(f"ybT{h}", [128, TOK], BF16) for h in range(8)]
    n_att = len(K.stack)
    dgT = sb("dgT", [128, 2048])
    K.dma(sp, dgT[:, :], c_dg[:, :], [], [dgT.buf])
    kTm = [[sb(f"kT{c}_{i}", [128, 2 * TOK], BF16) for i in range(2)] for c in range(2)]
    qTm = [[sb(f"qT{c}_{i}", [128, TOK], BF16) for i in range(2)] for c in range(2)]
    vT = [sb(f"vT{i}", [128, 32, 128], BF16) for i in range(2)]
    gT = [sb(f"gT{i}", [128, TOK], BF16) for i in range(2)]
    for c in range(2):
        for i in range(2):
            K.op(dve, lambda: V.memset(kTm[c][i][64:128, :], 0.0), [], [kTm[c][i].buf])
            K.op(dve, lambda: V.memset(qTm[c][i][64:128, :], 0.0), [], [qTm[c][i].buf])
            K.dma(pq, kTm[c][i][64:68, :], c_kpos[:, :], [], [kTm[c][i].buf])
    NSB = 6
    LAG = 4
    sbt = [sb(f"sbt{i}", [128, 512]) for i in range(3)]
    ptt = [sb(f"ptt{i}", [128, 512], BF16) for i in range(NSB)]
    ep_ = {n: sb("ep_" + n, [128, 512]) for n in ["r0", "t0", "r1", "t1", "d", "dsq", "rs", "y1"]}
    SC = [PS[0], PS[1], PS[2]]
    OT = [PS[3], PS[4]]
    LT = [PS[5], PS[6]]
    MSB = PS[7]

    ev = {n: sb("ev_" + n, [128, 512]) for n in ["o0", "o1", "l0", "l1"]}
    lacc = sb("lacc", [128, 512])

    def att_load(h):
        i = h % 2
        for c in range(2):
            K.dma(sp, kTm[c][i][0:64, :], kc[h][64 * c:64 * c + 64, :], [kc[h].buf], [kTm[c][i].buf])
            K.dma(sp, qTm[c][i][0:64, :], qb[h][64 * c:64 * c + 64, :], [qb[h].buf], [qTm[c][i].buf])
            K.dma(pq, qTm[c][i][64:68, :], c_qpos[4 * h:4 * h + 4, :], [], [qTm[c][i].buf])
        K.dma(sp, vT[i][:, :, :], vc[:, h * 128:(h + 1) * 128].rearrange("(t p) v -> p t v", p=128),
              [vc.buf], [vT[i].buf])
        K.dma(sp, gT[i][:, :], gb[h][:, :], [gb[h].buf], [gT[i].buf])

    def att_head(h):
        i = h % 2
        slope = 2.0 ** (-(h + 1))
        for qi in range(4):
            nfull = 16 + 4 * qi
            nkb = nfull + 4
            items = [(c, kb) for c in range(2) for kb in range(nkb)]
            n = len(items)

            def qk(j):
                c, kb = items[j]
                bank = SC[j % 3]
                K.mm(bank.ap[:, :], kTm[c][i][:, kb * 128:(kb + 1) * 128], qTm[c][i][:, qi * 512:(qi + 1) * 512], True, True,
                     [kTm[c][i].buf, qTm[c][i].buf], [bank.buf], inc=True)

            def soft(j):
                c, kb = items[j]
                bank = SC[j % 3]
                p_ = ptt[j % NSB]
                if kb < nfull:
                    if kb < 16:
                        K.actf(p_[:, :], bank.ap[:, :], AF.Exp, [bank.buf, cmask.buf], [p_.buf], bias=cmask[:, 0:1])
                    else:
                        K.actf(p_[:, :], bank.ap[:, :], AF.Exp, [bank.buf], [p_.buf])
                else:
                    s_ = sbt[j % 3]
                    bi = kb - nfull
                    K.op(dve, lambda: V.scalar_tensor_tensor(s_[:, :], dgT[:, bi * 512:(bi + 1) * 512], slope, bank.ap[:, :],
                                                             ALU.mult, ALU.add), [dgT.buf, bank.buf], [s_.buf])
                    K.actf(p_[:, :], s_[:, :], AF.Exp, [s_.buf], [p_.buf])

            def pvm(j):
                c, kb = items[j]
                p_ = ptt[j % NSB]
                K.mm(OT[c].ap[:, :], vT[i][:, kb, :], p_[:, :], kb == 0, kb == nkb - 1,
                     [vT[i].buf, p_.buf], [OT[c].buf], inc=(c == 1))
                if c == 0:
                    K.mm(LT[c].ap[:, :], ONESB, p_[:, :], kb == 0, kb == nkb - 1,
                         [cb.buf, p_.buf], [LT[c].buf], inc=True)
                else:
                    if kb == 0:
                        K.op(K.pool, lambda: nc.gpsimd.tensor_copy(lacc[:, :], p_[:, :]), [p_.buf], [lacc.buf])
                    else:
                        K.op(K.pool, lambda: nc.gpsimd.tensor_tensor(lacc[:, :], lacc[:, :], p_[:, :], ALU.add),
                             [lacc.buf, p_.buf], [lacc.buf])
                    if kb == nkb - 1:
                        K.mm(LT[1].ap[:, :], ONESF, lacc[:, :], True, True, [cf.buf, lacc.buf], [LT[1].buf], inc=True)

            for j in range(n + LAG):
                if j < n:
                    qk(j)
                if 1 <= j <= n:
                    soft(j - 1)
                if j >= LAG:
                    pvm(j - LAG)
                if pend[0] is not None and j in (12, 14):
                    if next(pend[0], "done") == "done":
                        pend[0] = None
            if pend[0] is not None:
                for _ in pend[0]:
                    pass
                pend[0] = None
            K.op(dve, lambda: V.tensor_copy(ev["l0"][:, :], LT[0].ap[:, :]), [LT[0].buf], [ev["l0"].buf])
            K.op(dve, lambda: V.tensor_copy(ev["o0"][:, :], OT[0].ap[:, :]), [OT[0].buf], [ev["o0"].buf])
            K.op(dve, lambda: V.tensor_copy(ev["l1"][:, :], LT[1].ap[:, :]), [LT[1].buf], [ev["l1"].buf])
            K.op(dve, lambda: V.tensor_copy(ev["o1"][:, :], OT[1].ap[:, :]), [OT[1].buf], [ev["o1"].buf])
            pend[0] = epilogue(h, i, qi)
            next(pend[0])
            if qi == 3:
                for _ in pend[0]:
                    pass
                pend[0] = None

    pend = [None]

    def epilogue(h, i, qi):
        e = ep_
        K.op(dve, lambda: V.reciprocal(e["r0"][:, :], ev["l0"][:, :]), [ev["l0"].buf], [e["r0"].buf])
        K.op(dve, lambda: V.tensor_tensor(e["t0"][:, :], ev["o0"][:, :], e["r0"][:, :], ALU.mult),
             [ev["o0"].buf, e["r0"].buf], [e["t0"].buf])
        K.op(dve, lambda: V.reciprocal(e["r1"][:, :], ev["l1"][:, :]), [ev["l1"].buf], [e["r1"].buf])
        K.op(dve, lambda: V.tensor_tensor(e["t1"][:, :], ev["o1"][:, :], e["r1"][:, :], ALU.mult),
             [ev["o1"].buf, e["r1"].buf], [e["t1"].buf])
        K.op(dve, lambda: V.scalar_tensor_tensor(e["d"][:, :], e["t1"][:, :], NEGLAM, e["t0"][:, :], ALU.mult, ALU.add),
             [e["t1"].buf, e["t0"].buf, pv.buf], [e["d"].buf])
        yield
        K.actf(e["dsq"][:, :], e["d"][:, :], AF.Square, [e["d"].buf], [e["dsq"].buf])
        K.mm(MSB.ap[:, :], O128, e["dsq"][:, :], True, True, [cf.buf, e["dsq"].buf], [MSB.buf], inc=True)
        yield
        K.actf(e["dsq"][:, :], MSB.ap[:, :], AF.Ln, [MSB.buf], [e["dsq"].buf], bias=EPS)
        K.actf(e["rs"][:, :], e["dsq"][:, :], AF.Exp, [e["dsq"].buf], [e["rs"].buf], scale=-0.5)
        K.op(dve, lambda: V.scalar_tensor_tensor(e["y1"][:, :], e["d"][:, :], SUBLN, e["rs"][:, :], ALU.mult, ALU.mult),
             [e["d"].buf, e["rs"].buf, pv.buf], [e["y1"].buf])
        K.op(dve, lambda: V.tensor_tensor(ybT[h][:, qi * 512:(qi + 1) * 512], e["y1"][:, :],
                                          gT[i][:, qi * 512:(qi + 1) * 512], ALU.mult),
             [e["y1"].buf, gT[i].buf], [ybT[h].buf])

    att_load(0)
    for h in range(8):
        if h + 1 < 8:
            att_load(h + 1)
        att_head(h)

    if debug:
        dbg["ybT0"] = ybT[0]

    K.barrier()
    while len(K.stack) > n_att:
        K.stack.pop().__exit__(None, None, None)

    mixT = sb("mixT", [128, 16, TOK], BF16)
    yaT = [sb(f"yaT{h}", [128, TOK], BF16) for h in range(8)]
    for h in range(8):
        K.dma(sp, yaT[h][:, :], yas[h][:, :], [yas[h].buf], [yaT[h].buf])
    mbuf = [Buf(f"mx{n}") for n in range(4)]
    wab = [sb(f"wab{i}", [128, 2, 8, 128], BF16) for i in range(2)]
    gab = [sb(f"gab{i}", [128, 2, 512], BF16) for i in range(4)]
    m12 = [sb(f"m12{i}", [128, 2, 512]) for i in range(2)]
    wob = [sb(f"wob{i}", [128, 16, 512], BF16) for i in range(2)]
    xres = [sb(f"xres{i}", [128, 512]) for i in range(4)]
    osb = [sb(f"osb{i}", [128, 512]) for i in range(4)]
    def load_wo(cg):
        wg_ = wob[cg % 2]
        K.dma(pq, wg_[:, :, :], wo[:, cg * 512:(cg + 1) * 512].rearrange("(kc p) n -> p kc n", p=128), [], [wg_.buf])

    cnt = 0
    for j in range(16):
        if j in (4, 8):
            load_wo(j // 4 - 1)
        wj = wab[j % 2]
        K.dma(pq, wj[:, 0, :, :], wa[:, j * 128:(j + 1) * 128].rearrange("(kc p) n -> p kc n", p=128), [], [wj.buf])
        K.dma(pq, wj[:, 1, :, :], wb[:, j * 128:(j + 1) * 128].rearrange("(kc p) n -> p kc n", p=128), [], [wj.buf])
        for n in range(4):
            gj = gab[cnt % 4]
            mj = m12[cnt % 2]
            K.dma(K.aq, gj[:, 0, :], gs[j][:, n * 512:(n + 1) * 512], [gs[j].buf], [gj.buf])
            K.dma(K.aq, gj[:, 1, :], gs[16 + j][:, n * 512:(n + 1) * 512], [gs[16 + j].buf], [gj.buf])
            pa, pb_ = PS[(2 * cnt) % 4], PS[(2 * cnt + 1) % 4]
            for k in range(8):
                K.mm(pa.ap[:, :], wj[:, 0, k, :], yaT[k][:, n * 512:(n + 1) * 512], k == 0, k == 7,
                     [wj.buf, yaT[k].buf], [pa.buf], inc=(k == 7))
            for k in range(8):
                K.mm(pb_.ap[:, :], wj[:, 1, k, :], ybT[k][:, n * 512:(n + 1) * 512], k == 0, k == 7,
                     [wj.buf, ybT[k].buf], [pb_.buf], inc=(k == 7))
            K.op(dve, lambda: V.tensor_tensor(mj[:, 0, :], pa.ap[:, :], gj[:, 0, :], ALU.mult), [pa.buf, gj.buf], [mj.buf])
            K.op(dve, lambda: V.tensor_tensor(mj[:, 1, :], pb_.ap[:, :], gj[:, 1, :], ALU.mult), [pb_.buf, gj.buf], [mj.buf])
            K.op(dve, lambda: V.tensor_tensor(mixT[:, j, n * 512:(n + 1) * 512], mj[:, 0, :], mj[:, 1, :], ALU.add),
                 [mj.buf], [mbuf[n]])
            cnt += 1
    out_toks = []
    cnt = 0
    for cg in range(4):
        wg = wob[cg % 2]
        if cg >= 2:
            load_wo(cg)
        for t in range(16):
            xr = xres[cnt % 4]
            ob_ = osb[cnt % 4]
            bank = PS[4 + cnt % 4]
            K.dma(K.aq, xr[:, :], xo[t * 128:(t + 1) * 128, cg * 512:(cg + 1) * 512], [], [xr.buf])
            for k in range(16):
                K.mm(bank.ap[:, :], mixT[:, k, t * 128:(t + 1) * 128], wg[:, k, :], k == 0, k == 15,
                     [mbuf[t // 4], wg.buf], [bank.buf], inc=(k == 15))
            K.op(dve, lambda: V.tensor_tensor(ob_[:, :], bank.ap[:, :], xr[:, :], ALU.add), [bank.buf, xr.buf], [ob_.buf])
            out_toks.append(K.dma(sp, y[t * 128:(t + 1) * 128, cg * 512:(cg + 1) * 512], ob_[:, :], [ob_.buf], []))
            cnt += 1

    dbg_out = {}
    if debug:
        for name, t_ in dbg.items():
            src_ = t_[:, :]
            o = nc.dram_tensor("dbg_" + name, list(src_.shape), src_.dtype, kind="ExternalOutput").ap()
            out_toks.append(K.dma(sp, o, src_, [t_.buf], []))
            dbg_out[name] = "dbg_" + name
    for t_ in K.last_tokens():
        if t_.eng.dma_sems is not None:
            sp.wait(t_)
    K.close()
    return nc, dbg_out


def _prep(inputs):
    f = lambda a: np.ascontiguousarray(np.asarray(a, dtype=np.float32))
    x = f(inputs["x"])
    w_in = f(inputs["w_in"])[0]
    perm = []
    for h in range(8):
        for blk in range(4):
            perm.extend(range(blk * 1024 + h * 128, blk * 1024 + (h + 1) * 128))
    perm.extend(range(4096, NIN))
    w_perm = np.ascontiguousarray(w_in[:, np.array(perm)])
    cf, cb, kpos, qpos, dg = _consts()
    shared = {
        "w_in": w_perm,
        "wa": f(inputs["w_branch_a"])[0], "wb": f(inputs["w_branch_b"])[0], "wo": f(inputs["w_out"])[0],
        "norm_w": f(inputs["norm_w"])[0], "alb": f(inputs["a_lower_bound"]), "aon": f(inputs["a_out_norm"])[0],
        "bqn": f(inputs["b_q_norm"])[0], "bkn": f(inputs["b_k_norm"])[0],
        "blam": f(inputs["b_lambda"])[0].reshape(256), "bsub": f(inputs["b_subln"])[0],
        "c_f": cf, "c_b": cb, "c_kpos": kpos, "c_qpos": qpos.reshape(32, TOK), "c_dg": dg,
    }
    zeros = np.zeros((TOK, D), np.float32)
    in_maps = []
    for c in range(NCORES):
        b, s = c // 2, c % 2
        m = dict(shared)
        m["xo"] = np.ascontiguousarray(x[b, s * TOK:(s + 1) * TOK])
        m["xp"] = np.ascontiguousarray(x[b, 0:TOK]) if s == 1 else zeros
        m["c_mask"] = np.full((128, 1), 0.0 if s == 1 else -1.0e9, np.float32)
        in_maps.append(m)
    return in_maps


def kernel(**inputs):
    in_maps = _prep(inputs)
    nc, _ = build(debug=False)
    res = run_bass_kernel_spmd(nc, in_maps, core_ids=list(range(NCORES)))
    out = np.empty((4, 2 * TOK, D), np.float32)
    for c in range(NCORES):
        b, s = c // 2, c % 2
        out[b, s * TOK:(s + 1) * TOK] = res.results[c]["y"]
    return out
```

```python
import numpy as np
import concourse.bass as bass
import concourse.mybir as mybir
from concourse.bass_utils import run_bass_kernel_spmd

F32 = mybir.dt.float32
BF16 = mybir.dt.bfloat16
AF = mybir.ActivationFunctionType
ALU = mybir.AluOpType
AX = mybir.AxisListType

D = 2048
TOK = 2048
NIN = 12288
EPS = 1e-6
NCORES = 8
LAMBDA_INIT = 0.2
NDMA_SEMS = 8

C_A = 0
C_BQ = 4096
C_BK = 5120
C_BV = 6144
C_BG = 7168
C_GA = 8192
C_GB = 10240


class Tok:
    __slots__ = ("sem", "val", "eng")

    def __init__(self, sem, val, eng):
        self.sem, self.val, self.eng = sem, val, eng


class Buf:
    __slots__ = ("name", "w", "r")

    def __init__(self, name=""):
        self.name, self.w, self.r = name, None, {}


class T:
    def __init__(self, ap, name=""):
        self.ap = ap
        self.buf = Buf(name)

    def __getitem__(self, k):
        return self.ap[k]


class Eng:
    def __init__(self, name, h, sem=None, dma_sems=None, seen=None):
        self.name, self.h, self.sem = name, h, sem
        self.count = 0
        self.dma_sems = dma_sems
        self.n = 0
        self.seen = {} if seen is None else seen
        self.pending = []

    def wait(self, tok):
        assert tok.val is not None, "waiting on an unresolved PE token"
        if self.seen.get(tok.sem, 0) >= tok.val:
            return
        self.h.wait_ge(tok.sem, tok.val)
        self.seen[tok.sem] = tok.val


class Kern:
    def __init__(self, nc):
        self.nc = nc
        self.stack = []
        mk = lambda n: self.enter(nc.semaphore(n))
        self.pe = Eng("pe", nc.tensor, mk("s_pe"))
        self.act = Eng("act", nc.scalar, mk("s_act"))
        self.dve = Eng("dve", nc.vector, mk("s_dve"))
        self.sp = Eng("sp", nc.sync, dma_sems=[mk(f"s_sp{i}") for i in range(NDMA_SEMS)])
        self.pq = Eng("pq", nc.gpsimd, dma_sems=[mk(f"s_pq{i}") for i in range(NDMA_SEMS)])
        self.aq = Eng("aq", nc.scalar, dma_sems=[mk(f"s_aq{i}") for i in range(NDMA_SEMS)], seen=self.act.seen)
        self.engines = [self.pe, self.act, self.dve, self.sp, self.pq, self.aq]

    def enter(self, cm):
        v = cm.__enter__()
        self.stack.append(cm)
        return v

    def close(self):
        while self.stack:
            self.stack.pop().__exit__(None, None, None)

    def op(self, eng, fn, reads=(), writes=(), inc=True):
        is_dma = eng.dma_sems is not None
        for b in reads:
            w = b.w
            if w is not None and not (w.eng is eng and eng.name == "pe"):
                eng.wait(w)
        for b in writes:
            w = b.w
            if w is not None and (is_dma or w.eng is not eng):
                eng.wait(w)
            for t in b.r.values():
                if is_dma or t.eng is not eng:
                    eng.wait(t)
        if is_dma:
            slot = eng.n % NDMA_SEMS
            rnd = eng.n // NDMA_SEMS
            sem = eng.dma_sems[slot]
            if rnd > 0 and eng.seen.get(sem, 0) < 16 * rnd:
                eng.h.wait_ge(sem, 16 * rnd)
                eng.seen[sem] = 16 * rnd
            ins = fn()
            ins.then_inc(sem, 16)
            tok = Tok(sem, 16 * (rnd + 1), eng)
            eng.n += 1
        else:
            ins = fn()
            if inc:
                eng.count += 1
                ins.then_inc(eng.sem, 1)
                tok = Tok(eng.sem, eng.count, eng)
                for p in eng.pending:
                    p.val = eng.count
                eng.pending = []
            else:
                tok = Tok(eng.sem, None, eng)
                eng.pending.append(tok)
        for b in reads:
            b.r[(tok.sem, id(eng))] = tok
        for b in writes:
            b.w = tok
            b.r = {}
        return tok

    def last_tokens(self):
        toks = []
        for e in self.engines:
            if e.dma_sems is None:
                assert not e.pending
                if e.count:
                    toks.append(Tok(e.sem, e.count, e))
            else:
                for i, sem in enumerate(e.dma_sems):
                    k = (e.n - i + NDMA_SEMS - 1) // NDMA_SEMS
                    if k > 0:
                        toks.append(Tok(sem, 16 * k, e))
        return toks

    def barrier(self):
        toks = self.last_tokens()
        for e in self.engines:
            for t in toks:
                if t.eng is e and e.dma_sems is None:
                    continue
                e.wait(t)

    def mm(self, out, lhsT, rhs, start, stop, reads, writes, inc):
        return self.op(self.pe, lambda: self.nc.tensor.matmul(out, lhsT, rhs, start=start, stop=stop),
                       reads, writes, inc)

    def tr(self, out, in_, ident, reads, writes, inc=True):
        return self.op(self.pe, lambda: self.nc.tensor.transpose(out, in_, ident), reads, writes, inc)

    def actf(self, out, in_, func, reads, writes, bias=0.0, scale=1.0, accum_out=None):
        def f():
            kw = {}
            if accum_out is not None:
                kw["accum_out"] = accum_out
            return self.nc.scalar.activation(out, in_, func, bias=bias, scale=scale, **kw)
        return self.op(self.act, f, reads, writes)

    def dma(self, eng, out, in_, reads, writes):
        return self.op(eng, lambda: eng.h.dma_start(out=out, in_=in_), reads, writes)


def _consts():
    i = np.arange(128)[:, None]
    j = np.arange(128)[None, :]
    trip = (i <= j).astype(np.float32) - (i <= 63).astype(np.float32)
    triu = (i <= j).astype(np.float32)
    bd = ((i // 64) == (j // 64)).astype(np.float32) / 64.0
    o128 = np.full((128, 128), 1.0 / 128.0, np.float32)
    sel = np.concatenate([(i <= 63), (i >= 64)], axis=1).astype(np.float32)
    cf = np.concatenate([trip, triu, bd, o128, sel], axis=1).astype(np.float32)
    cb = np.concatenate([np.eye(128, dtype=np.float32), np.ones((128, 128), np.float32), trip, sel], axis=1)
    tk = np.arange(4096)
    kpos = np.stack([64.0 * (tk // 64), (tk % 64).astype(np.float64), np.ones(4096), np.ones(4096)]).astype(np.float32)
    tq = TOK + np.arange(TOK)
    qpos = np.zeros((8, 4, TOK), np.float32)
    for h in range(8):
        sl = 2.0 ** (-(h + 1))
        qpos[h] = np.stack([np.full(TOK, sl), np.full(TOK, sl), -sl * 64.0 * (tq // 64), -sl * (tq % 64)])
    jq = np.arange(512)[None, :]
    dg = []
    for bi in range(4):
        kp = 128 * bi + i
        allowed = (kp // 64) <= (jq // 64)
        fix = np.where(kp > jq, -2.0 * (kp - jq), 0.0)
        dg.append(np.where(allowed, fix, -1.0e9).astype(np.float32))
    dg = np.concatenate(dg, axis=1)
    return cf, cb, kpos, qpos, dg


def build(debug=False):
    nc = bass.Bass("TRN2", target_bir_lowering=False)
    K = Kern(nc)
    pe, act, dve, sp, pq = K.pe, K.act, K.dve, K.sp, K.pq
    V = nc.vector
    dt_in = lambda n, s: nc.dram_tensor(n, s, F32, kind="ExternalInput").ap()
    xo = dt_in("xo", [TOK, D])
    xp = dt_in("xp", [TOK, D])
    w_in = dt_in("w_in", [D, NIN])
    wa = dt_in("wa", [1024, D])
    wb = dt_in("wb", [1024, D])
    wo = dt_in("wo", [D, D])
    norm_w = dt_in("norm_w", [D])
    alb = dt_in("alb", [2, 1024])
    aon = dt_in("aon", [128])
    bqn = dt_in("bqn", [64])
    bkn = dt_in("bkn", [64])
    blam = dt_in("blam", [256])
    bsub = dt_in("bsub", [128])
    c_f = dt_in("c_f", [128, 514])
    c_b = dt_in("c_b", [128, 386])
    c_kpos = dt_in("c_kpos", [4, 4096])
    c_qpos = dt_in("c_qpos", [32, TOK])
    c_dg = dt_in("c_dg", [128, 2048])
    c_mask = dt_in("c_mask", [128, 1])
    y = nc.dram_tensor("y", [TOK, D], F32, kind="ExternalOutput").ap()
    scr = lambda n, s: T(nc.dram_tensor(n, s, BF16, kind="Internal").ap(), n)
    kc = [scr(f"kc{h}", [128, 2 * TOK]) for h in range(8)]
    vc = scr("vc", [2 * TOK, 1024])
    qb = [scr(f"qb{h}", [128, TOK]) for h in range(8)]
    gb = [scr(f"gb{h}", [128, TOK]) for h in range(8)]
    gs = [scr(f"gs{j}", [128, TOK]) for j in range(32)]
    yas = [scr(f"yas{h}", [128, TOK]) for h in range(8)]
    dbg = {}

    def sb(name, shape, dt=F32):
        return T(K.enter(nc.sbuf_tensor(name, shape, dt)), name)

    PS = [T(K.enter(nc.psum_tensor(f"ps{i}", [128, 512], F32)), f"ps{i}") for i in range(8)]

    cf = sb("cf", [128, 514])
    cb = sb("cb", [128, 386], BF16)
    cmask = sb("cmask", [128, 1])
    K.dma(sp, cf[:, :], c_f[:, :], [], [cf.buf])
    K.dma(pq, cb[:, :], c_b[:, :], [], [cb.buf])
    K.dma(sp, cmask[:, :], c_mask[:, :], [], [cmask.buf])
    TRIP, TRIU, BD, O128, SEL = (cf[:, 0:128], cf[:, 128:256], cf[:, 256:384], cf[:, 384:512], cf[:, 512:514])
    IDENT, ONESB, TRIPB, SELB = cb[:, 0:128], cb[:, 128:256], cb[:, 256:384], cb[:, 384:386]

    gainA = sb("gainA", [128, 128])
    K.dma(sp, gainA[:, :], aon.partition_broadcast(128), [], [gainA.buf])
    pv = sb("pv", [128, 16])
    pvr = sb("pvr", [128, 4])
    with nc.allow_non_contiguous_dma(reason="tiny parameter vectors"):
        for half in range(2):
            K.dma(sp, pvr[64 * half:64 * half + 64, 0:1], bqn.rearrange("(p o) -> p o", o=1), [], [pvr.buf])
            K.dma(sp, pvr[64 * half:64 * half + 64, 1:2], bkn.rearrange("(p o) -> p o", o=1), [], [pvr.buf])
        K.dma(sp, pvr[:, 2:3], bsub.rearrange("(p o) -> p o", o=1), [], [pvr.buf])
    lam4 = sb("lam4", [128, 256])
    K.dma(sp, lam4[:, :], blam.partition_broadcast(128), [], [lam4.buf])
    lamt = sb("lamt", [128, 128])
    K.op(dve, lambda: V.tensor_scalar(pv[:, 0:1], pvr[:, 0:1], 0.125, None, ALU.mult), [pvr.buf], [pv.buf])
    K.op(dve, lambda: V.tensor_copy(pv[:, 1:2], pvr[:, 1:2]), [pvr.buf], [pv.buf])
    K.op(dve, lambda: V.tensor_scalar(pv[:, 2:3], pvr[:, 2:3], 1.0 - LAMBDA_INIT, None, ALU.mult), [pvr.buf], [pv.buf])
    K.op(dve, lambda: V.tensor_tensor(lamt[:, 0:64], lam4[:, 0:64], lam4[:, 64:128], ALU.mult), [lam4.buf], [lamt.buf])
    K.op(dve, lambda: V.tensor_tensor(lamt[:, 64:128], lam4[:, 128:192], lam4[:, 192:256], ALU.mult), [lam4.buf], [lamt.buf])
    K.op(dve, lambda: V.reduce_sum(pv[:, 4:5], lamt[:, 0:64], axis=AX.X), [lamt.buf], [pv.buf])
    K.op(dve, lambda: V.reduce_sum(pv[:, 5:6], lamt[:, 64:128], axis=AX.X), [lamt.buf], [pv.buf])
    K.actf(pv[:, 6:8], pv[:, 4:6], AF.Exp, [pv.buf], [pv.buf])
    K.op(dve, lambda: V.tensor_tensor(pv[:, 8:9], pv[:, 7:8], pv[:, 6:7], ALU.subtract), [pv.buf], [pv.buf])
    K.op(dve, lambda: V.tensor_scalar(pv[:, 3:4], pv[:, 8:9], -LAMBDA_INIT, None, ALU.add), [pv.buf], [pv.buf])
    GQ, GK, SUBLN, NEGLAM = pv[:, 0:1], pv[:, 1:2], pv[:, 2:3], pv[:, 3:4]


    n_proj = len(K.stack)
    hT = sb("hT", [128, 16, TOK], BF16)
    hbuf = [[Buf(f"h{t}_{hf}") for hf in range(2)] for t in range(16)]
    nwB = sb("nwB", [128, D])
    K.dma(sp, nwB[:, :], norm_w.partition_broadcast(128), [], [nwB.buf])
    omlB = sb("omlB", [128, 1024])

    Wt = [sb(f"Wt{i}", [128, 16, 512], BF16) for i in range(2)]
    wcnt = [0]
    xt = [sb(f"xt{i}", [128, D]) for i in range(3)]
    xn = sb("xn", [128, D], BF16)
    junk = xn
    a01 = xt[0]
    K.dma(sp, a01[:, 0:1024], alb[0, :].partition_broadcast(128), [], [a01.buf])
    K.dma(sp, a01[:, 1024:2048], alb[1, :].partition_broadcast(128), [], [a01.buf])
    K.op(dve, lambda: V.tensor_tensor(a01[:, 0:1024], a01[:, 1024:2048], a01[:, 0:1024], ALU.subtract), [a01.buf], [a01.buf])
    K.actf(omlB[:, :], a01[:, 0:1024], AF.Sigmoid, [a01.buf], [omlB.buf])
    K.op(dve, lambda: V.tensor_scalar(omlB[:, :], omlB[:, :], -0.5, None, ALU.mult), [omlB.buf], [omlB.buf])
    st1 = sb("st1", [128, 8])
    Sst = [sb(f"S{h}", [128, 128]) for h in range(8)]
    stg = [sb(f"stg{i}", [128, 512], BF16) for i in range(4)]
    stgc = [0]
    nsq = [sb(f"nsq{i}", [128, 512]) for i in range(2)]
    nrs = [sb(f"nrs{i}", [128, 512]) for i in range(2)]
    ncnt = [0]
    raw = [sb(f"raw{i}", [128, 4, 512]) for i in range(2)]
    vbq = [sb(f"vbq{i}", [128, 4, 128], BF16) for i in range(2)]
    hq = {n: sb("hq_" + n, shp, dt) for n, shp, dt in [
        ("lf", [128, 4, 128], F32), ("lfh", [128, 4, 128], BF16), ("lfl", [128, 4, 128], BF16), ("ep", [128, 4, 128], F32), ("en", [128, 4, 128], F32),
        ("qt", [128, 4, 128], BF16), ("kt", [128, 4, 128], BF16), ("qkT", [128, 1024], BF16),
        ("pm", [128, 4, 128], BF16), ("ug", [128, 4, 128], F32), ("sa", [128, 5, 128], F32),
        ("sp", [128, 4, 128], BF16), ("osq", [128, 4, 128], F32), ("ya", [128, 4, 128], BF16),
        ("ebc", [128, 4, 2], F32), ("gc", [128, 4], F32), ("ss", [128, 4], F32), ("lnv", [128, 4], F32),
        ("rstd", [128, 4], F32), ("yat", [128, 512], BF16), ("gg", [128, 4, 128], F32)]}

    def build_front(t):
        if True:
            x_t = xt[t % 3]
            K.actf(junk[:, :], x_t[:, :], AF.Square, [x_t.buf], [xn.buf, st1.buf], accum_out=st1[:, 0:1])
            K.actf(st1[:, 1:2], st1[:, 0:1], AF.Sqrt, [st1.buf], [st1.buf], bias=EPS, scale=1.0 / D)
            K.op(dve, lambda: V.reciprocal(st1[:, 2:3], st1[:, 1:2]), [st1.buf], [st1.buf])
            K.op(dve, lambda: V.scalar_tensor_tensor(xn[:, :], x_t[:, :], st1[:, 2:3], nwB[:, :], ALU.mult, ALU.mult),
                 [x_t.buf, st1.buf, nwB.buf], [xn.buf])

    def build_back(t):
        if True:
            for half in range(2):
                bank = PS[half]
                pb = bank.ap[:, :].bitcast(BF16)
                for c in range(8):
                    kcix = half * 8 + c
                    K.tr(pb[:, c * 128:(c + 1) * 128], xn[:, kcix * 128:(kcix + 1) * 128], IDENT,
                         [xn.buf, cb.buf], [bank.buf], inc=(c == 7))
                src = pb.rearrange("p (a b) -> p a b", b=128)
                dst = hT[:, half * 8:half * 8 + 8, t * 128:(t + 1) * 128]
                if half == 0:
                    K.op(dve, lambda: V.tensor_copy(dst, src), [bank.buf], [hbuf[t][0]])
                else:
                    K.op(act, lambda: nc.scalar.copy(dst, src), [bank.buf], [hbuf[t][1]])

    def load_w(c0, ncols):
        w = Wt[wcnt[0] % 2]
        wcnt[0] += 1
        src = w_in[:, c0:c0 + ncols].rearrange("(kc p) n -> p kc n", p=128)
        K.dma(pq, w[:, :, 0:ncols], src, [], [w.buf])
        return w

    def gemm_fm(w, col, n, bank):
        for k in range(16):
            K.mm(bank.ap[:, :], w[:, k, col:col + 128], hT[:, k, n * 512:(n + 1) * 512], k == 0, k == 15,
                 [w.buf] + sum(hbuf[4 * n:4 * n + 4], []), [bank.buf], inc=(k == 15))

    def gemm_tm(w, col, ncols, t, bank):
        for k in range(16):
            K.mm(bank.ap[:, 0:ncols], hT[:, k, t * 128:(t + 1) * 128], w[:, k, col:col + ncols], k == 0, k == 15,
                 [w.buf] + hbuf[t], [bank.buf], inc=(k == 15))

    def next_stg():
        s = stg[stgc[0] % 4]
        stgc[0] += 1
        return s

    def qknorm(bank, gain, dst_dram, dst_ap):
        i = ncnt[0] % 2
        ncnt[0] += 1
        sq, rs = nsq[i], nrs[i]
        K.actf(sq[:, :], bank.ap[:, :], AF.Square, [bank.buf], [sq.buf])
        msb = PS[4]
        K.mm(msb.ap[:, :], BD, sq[:, :], True, True, [cf.buf, sq.buf], [msb.buf], inc=True)
        K.actf(rs[:, :], msb.ap[:, :], AF.Sqrt, [msb.buf], [rs.buf], bias=EPS)
        K.op(dve, lambda: V.reciprocal(rs[:, :], rs[:, :]), [rs.buf], [rs.buf])
        s = next_stg()
        K.op(dve, lambda: V.scalar_tensor_tensor(s[:, :], bank.ap[:, :], gain, rs[:, :], ALU.mult, ALU.mult),
             [bank.buf, rs.buf, pv.buf], [s.buf])
        K.dma(sp, dst_ap, s[:, :], [s.buf], [dst_dram.buf])

    gcnt = [0]

    def next_bank():
        b = PS[2 + gcnt[0] % 2]
        gcnt[0] += 1
        return b

    def proj_k(tok_off):
        for g in range(2):
            w = load_w(C_BK + 512 * g, 512)
            for m in range(4):
                h = 4 * g + m
                for n in range(4):
                    bank = next_bank()
                    gemm_fm(w, 128 * m, n, bank)
                    qknorm(bank, GK, kc[h], kc[h][:, tok_off + n * 512: tok_off + (n + 1) * 512])

    def proj_q():
        for g in range(2):
            w = load_w(C_BQ + 512 * g, 512)
            for m in range(4):
                h = 4 * g + m
                for n in range(4):
                    bank = next_bank()
                    gemm_fm(w, 128 * m, n, bank)
                    qknorm(bank, GQ, qb[h], qb[h][:, n * 512:(n + 1) * 512])

    def proj_act_fm(c0, func, dsts):
        ngroups = len(dsts) // 4
        for g in range(ngroups):
            w = load_w(c0 + 512 * g, 512)
            for m in range(4):
                for n in range(4):
                    bank = next_bank()
                    gemm_fm(w, 128 * m, n, bank)
                    s = next_stg()
                    K.actf(s[:, :], bank.ap[:, :], func, [bank.buf], [s.buf])
                    d = dsts[4 * g + m]
                    K.dma(sp, d[:, n * 512:(n + 1) * 512], s[:, :], [s.buf], [d.buf])

    def build_and_v(xsrc, tok_off):
        ws = [load_w(C_BV + 512 * g, 512) for g in range(2)]
        def load_x(t):
            K.dma(sp, xt[t % 3][:, :], xsrc[t * 128:(t + 1) * 128, :], [], [xt[t % 3].buf])
        load_x(0)
        load_x(1)
        load_x(2)
        build_front(0)
        build_back(0)
        for t in range(16):
            if t + 1 < 16:
                build_front(t + 1)
                if t + 3 < 16:
                    load_x(t + 3)
            for g in range(2):
                w = ws[g]
                bank = next_bank()
                gemm_tm(w, 0, 512, t, bank)
                s = next_stg()
                K.op(dve, lambda: V.tensor_copy(s[:, :], bank.ap[:, :]), [bank.buf], [s.buf])
                K.dma(sp, vc[tok_off + t * 128: tok_off + (t + 1) * 128, g * 512:(g + 1) * 512], s[:, :],
                      [s.buf], [vc.buf])
            if t + 1 < 16:
                build_back(t + 1)

    GB = [PS[0], PS[1], PS[2], PS[3]]
    X1, X2, X3, X4 = PS[4], PS[5], PS[6], PS[7]

    def bc4(ap2d):
        return ap2d.unsqueeze(1).broadcast_to([128, 4, 128])

    def pass1(P, vb, own):
        zc, ic = (1, 2) if own else (0, 1)
        z = P[:, :, zc * 128:(zc + 1) * 128]
        K.actf(z, z, AF.Tanh, [P.buf], [P.buf], scale=0.5)
        if own:
            q = P[:, :, 0:128]
            gg = P[:, :, 384:512]
            K.actf(q, q, AF.Silu, [P.buf], [P.buf])
            K.actf(gg, gg, AF.Silu, [P.buf], [P.buf])
        K.actf(vb[:, :, :], P[:, :, ic * 128:(ic + 1) * 128], AF.Copy, [P.buf], [vb.buf])

    def pass2(h, qd, P, vb, own):
        g = hq
        S = Sst[h]
        zc = 1 if own else 0
        kk = P[:, :, zc * 128:(zc + 1) * 128]
        K.op(dve, lambda: V.scalar_tensor_tensor(kk, kk, 1.0, bc4(omlB[:, h * 128:(h + 1) * 128]), ALU.subtract, ALU.mult),
             [P.buf, omlB.buf], [P.buf])
        K.actf(g["lf"][:, :, :], kk, AF.Ln, [P.buf], [g["lf"].buf], bias=1.0, scale=-1.0)
        K.op(dve, lambda: V.tensor_copy(g["lfh"][:, :, :], g["lf"][:, :, :]), [g["lf"].buf], [g["lfh"].buf])
        K.op(dve, lambda: V.tensor_tensor(g["lfl"][:, :, :], g["lf"][:, :, :], g["lfh"][:, :, :], ALU.subtract),
             [g["lf"].buf, g["lfh"].buf], [g["lfl"].buf])
        yield
        for t in range(4):
            K.mm(X1.ap[:, t * 128:(t + 1) * 128], TRIPB, g["lfh"][:, t, :], True, False, [cb.buf, g["lfh"].buf], [X1.buf], inc=False)
            K.mm(X1.ap[:, t * 128:(t + 1) * 128], TRIPB, g["lfl"][:, t, :], False, True, [cb.buf, g["lfl"].buf], [X1.buf], inc=(t == 3))
        for t in range(4):
            K.mm(X2.ap[:, 2 * t:2 * t + 2], g["lfh"][:, t, :], SELB, True, False, [cb.buf, g["lfh"].buf], [X2.buf], inc=False)
            K.mm(X2.ap[:, 2 * t:2 * t + 2], g["lfl"][:, t, :], SELB, False, True, [cb.buf, g["lfl"].buf], [X2.buf], inc=(t == 3))
        yield
        x1v = X1.ap[:, :].rearrange("p (a b) -> p a b", b=128)
        K.actf(g["en"][:, :, :], x1v, AF.Exp, [X1.buf], [g["en"].buf], scale=-1.0)
        K.actf(g["ebc"][:, :, :], X2.ap[:, 0:8].rearrange("p (a b) -> p a b", b=2), AF.Exp, [X2.buf], [g["ebc"].buf])
        if own:
            K.actf(g["ep"][:, :, :], x1v, AF.Exp, [X1.buf], [g["ep"].buf])
        K.op(dve, lambda: V.tensor_tensor(g["kt"][:, :, :], kk, g["en"][:, :, :], ALU.mult), [P.buf, g["en"].buf], [g["kt"].buf])
        if own:
            K.op(dve, lambda: V.tensor_tensor(g["qt"][:, :, :], P[:, :, 0:128], g["ep"][:, :, :], ALU.mult),
                 [P.buf, g["ep"].buf], [g["qt"].buf])
            K.op(dve, lambda: V.tensor_tensor(g["gg"][:, :, :], P[:, :, 384:512], bc4(gainA[:, :]), ALU.mult),
                 [P.buf, gainA.buf], [g["gg"].buf])
        K.op(dve, lambda: V.tensor_tensor(g["gc"][:, 0:3], g["ebc"][:, 1:4, 0], g["ebc"][:, 0:3, 1], ALU.mult),
             [g["ebc"].buf], [g["gc"].buf])
        K.op(dve, lambda: V.tensor_copy(g["gc"][:, 3:4], g["ebc"][:, 3, 1:2]), [g["ebc"].buf], [g["gc"].buf])
        K.op(dve, lambda: V.tensor_scalar(g["sa"][:, 0, :], S[:, :], g["ebc"][:, 0, 0:1], None, ALU.mult),
             [S.buf, g["ebc"].buf], [g["sa"].buf])
        yield
        for t in range(4):
            K.mm(X3.ap[:, t * 128:(t + 1) * 128], g["kt"][:, t, :], vb[:, t, :], True, True, [g["kt"].buf, vb.buf], [X3.buf], inc=(t == 3))
        if own:
            x2b = X2.ap[:, :].bitcast(BF16)
            for t in range(4):
                K.tr(x2b[:, t * 128:(t + 1) * 128], g["qt"][:, t, :], IDENT, [g["qt"].buf, cb.buf], [X2.buf], inc=False)
            for t in range(4):
                K.tr(x2b[:, 512 + t * 128:512 + (t + 1) * 128], g["kt"][:, t, :], IDENT, [g["kt"].buf, cb.buf], [X2.buf], inc=(t == 3))
        yield
        K.op(dve, lambda: V.tensor_tensor(g["ug"][:, :, :], X3.ap[:, :].rearrange("p (a b) -> p a b", b=128),
                                          g["gc"][:, 0:4].unsqueeze(2).broadcast_to([128, 4, 128]), ALU.mult),
             [X3.buf, g["gc"].buf], [g["ug"].buf])
        if own:
            K.actf(g["qkT"][:, :], x2b[:, :], AF.Copy, [X2.buf], [g["qkT"].buf])
        yield
        for t in range(4):
            dst = g["sa"][:, t + 1, :] if t < 3 else S[:, :]
            dbuf = g["sa"].buf if t < 3 else S.buf
            K.op(dve, lambda: V.scalar_tensor_tensor(dst, g["sa"][:, t, :], g["gc"][:, t:t + 1], g["ug"][:, t, :], ALU.mult, ALU.add),
                 [g["sa"].buf, g["gc"].buf, g["ug"].buf], [dbuf])
        if not own:
            return
        K.actf(g["sp"][:, :, :], g["sa"][:, 0:4, :], AF.Copy, [g["sa"].buf], [g["sp"].buf])
        for t in range(4):
            K.mm(X1.ap[:, t * 128:(t + 1) * 128], g["qkT"][:, 512 + t * 128:512 + (t + 1) * 128], g["qkT"][:, t * 128:(t + 1) * 128],
                 True, True, [g["qkT"].buf], [X1.buf], inc=(t == 3))
        yield
        K.op(dve, lambda: V.tensor_tensor(g["pm"][:, :, :], x1v, bc4(TRIU), ALU.mult), [X1.buf, cf.buf], [g["pm"].buf])
        yield
        for t in range(4):
            K.mm(X4.ap[:, t * 128:(t + 1) * 128], g["pm"][:, t, :], vb[:, t, :], True, False, [g["pm"].buf, vb.buf], [X4.buf], inc=False)
            K.mm(X4.ap[:, t * 128:(t + 1) * 128], g["qkT"][:, t * 128:(t + 1) * 128], g["sp"][:, t, :], False, True,
                 [g["qkT"].buf, g["sp"].buf], [X4.buf], inc=(t == 3))
        yield
        x4v = X4.ap[:, :].rearrange("p (a b) -> p a b", b=128)
        K.actf(g["osq"][:, :, :], x4v, AF.Square, [X4.buf], [g["osq"].buf])
        K.op(dve, lambda: V.reduce_sum(g["ss"][:, :], g["osq"][:, :, :], axis=AX.X), [g["osq"].buf], [g["ss"].buf])
        K.actf(g["lnv"][:, :], g["ss"][:, :], AF.Ln, [g["ss"].buf], [g["lnv"].buf], bias=EPS, scale=1.0 / 128.0)
        K.actf(g["rstd"][:, :], g["lnv"][:, :], AF.Exp, [g["lnv"].buf], [g["rstd"].buf], scale=-0.5)
        for t in range(4):
            K.op(dve, lambda: V.scalar_tensor_tensor(g["ya"][:, t, :], X4.ap[:, t * 128:(t + 1) * 128], g["rstd"][:, t:t + 1],
                                                     g["gg"][:, t, :], ALU.mult, ALU.mult),
                 [X4.buf, g["rstd"].buf, g["gg"].buf], [g["ya"].buf])
        yield
        x3b = X3.ap[:, :].bitcast(BF16)
        for t in range(4):
            K.tr(x3b[:, t * 128:(t + 1) * 128], g["ya"][:, t, :], IDENT, [g["ya"].buf, cb.buf], [X3.buf], inc=(t == 3))
        yield
        K.actf(g["yat"][:, :], x3b[:, 0:512], AF.Copy, [X3.buf], [g["yat"].buf])
        K.dma(sp, yas[h][:, qd * 512:(qd + 1) * 512], g["yat"][:, :], [g["yat"].buf], [yas[h].buf])

    def proj_a(own):
        ncols = 512 if own else 256
        sched = [1, 2, 2, 2] if own else [1, 2, 2, 1]
        tsched = [2, 0, 0, 0]
        cur, tail = None, None
        qcount = 0

        def adv(gen, k):
            if gen is None:
                return None
            for _ in range(k):
                if next(gen, "done") == "done":
                    return None
            return gen

        for h in range(8):
            w = load_w(C_A + 512 * h + (0 if own else 128), ncols)
            for qd in range(4):
                P, vb = raw[qcount % 2], vbq[qcount % 2]
                qcount += 1
                for tt in range(4):
                    t = 4 * qd + tt
                    bank = GB[tt]
                    gemm_tm(w, 0, ncols, t, bank)
                    K.op(dve, lambda: V.tensor_copy(P[:, tt, 0:ncols], bank.ap[:, 0:ncols]), [bank.buf], [P.buf])
                    tail = adv(tail, tsched[tt])
                    if tt == 1 and tail is not None:
                        for _ in tail:
                            pass
                        tail = None
                    cur = adv(cur, sched[tt])
                if own:
                    cur = adv(cur, 2)
                tail = cur
                pass1(P, vb, own)
                cur = pass2(h, qd, P, vb, own)
        for gen in (tail, cur):
            if gen is not None:
                for _ in gen:
                    pass

    for h in range(8):
        K.op(dve, lambda: V.memset(Sst[h][:, :], 0.0), [], [Sst[h].buf])

    build_and_v(xp, 0)
    proj_k(0)
    proj_a(False)
    build_and_v(xo, TOK)
    proj_k(TOK)
    proj_q()
    proj_act_fm(C_BG, AF.Silu, gb)
    proj_act_fm(C_GA, AF.Sigmoid, gs)
    proj_a(True)


    K.barrier()
    while len(K.stack) > n_proj:
        K.stack.pop().__exit__(None, None, None)

    ybT = [sb(f"ybT{h}", [128, TOK], BF16) for h in range(8)]
    n_att = len(K.stack)
    dgT = sb("dgT", [128, 2048])
    K.dma(sp, dgT[:, :], c_dg[:, :], [], [dgT.buf])
    kTm = [[sb(f"kT{c}_{i}", [128, 2 * TOK], BF16) for i in range(2)] for c in range(2)]
    qTm = [[sb(f"qT{c}_{i}", [128, TOK], BF16) for i in range(2)] for c in range(2)]
    vT = [sb(f"vT{i}", [128, 32, 128], BF16) for i in range(2)]
    gT = [sb(f"gT{i}", [128, TOK], BF16) for i in range(2)]
    for c in range(2):
        for i in range(2):
            K.op(dve, lambda: V.memset(kTm[c][i][64:128, :], 0.0), [], [kTm[c][i].buf])
            K.op(dve, lambda: V.memset(qTm[c][i][64:128, :], 0.0), [], [qTm[c][i].buf])
            K.dma(pq, kTm[c][i][64:68, :], c_kpos[:, :], [], [kTm[c][i].buf])
    NSB = 6
    LAG = 4
    sbt = [sb(f"sbt{i}", [128, 512]) for i in range(3)]
    ptt = [sb(f"ptt{i}", [128, 512], BF16) for i in range(NSB)]
    ep_ = {n: sb("ep_" + n, [128, 512]) for n in ["r0", "t0", "r1", "t1", "d", "dsq", "rs", "y1"]}
    SC = [PS[0], PS[1], PS[2]]
    OT = [PS[3], PS[4]]
    LT = [PS[5], PS[6]]
    MSB = PS[7]

    ev = {n: sb("ev_" + n, [128, 512]) for n in ["o0", "o1", "l0", "l1"]}

    def att_load(h):
        i = h % 2
        for c in range(2):
            K.dma(sp, kTm[c][i][0:64, :], kc[h][64 * c:64 * c + 64, :], [kc[h].buf], [kTm[c][i].buf])
            K.dma(sp, qTm[c][i][0:64, :], qb[h][64 * c:64 * c + 64, :], [qb[h].buf], [qTm[c][i].buf])
            K.dma(pq, qTm[c][i][64:68, :], c_qpos[4 * h:4 * h + 4, :], [], [qTm[c][i].buf])
        K.dma(sp, vT[i][:, :, :], vc[:, h * 128:(h + 1) * 128].rearrange("(t p) v -> p t v", p=128),
              [vc.buf], [vT[i].buf])
        K.dma(sp, gT[i][:, :], gb[h][:, :], [gb[h].buf], [gT[i].buf])

    def att_head(h):
        i = h % 2
        slope = 2.0 ** (-(h + 1))
        for qi in range(4):
            nfull = 16 + 4 * qi
            nkb = nfull + 4
            items = [(c, kb) for c in range(2) for kb in range(nkb)]
            n = len(items)

            def qk(j):
                c, kb = items[j]
                bank = SC[j % 3]
                K.mm(bank.ap[:, :], kTm[c][i][:, kb * 128:(kb + 1) * 128], qTm[c][i][:, qi * 512:(qi + 1) * 512], True, True,
                     [kTm[c][i].buf, qTm[c][i].buf], [bank.buf], inc=True)

            def soft(j):
                c, kb = items[j]
                bank = SC[j % 3]
                p_ = ptt[j % NSB]
                if kb < nfull:
                    if kb < 16:
                        K.actf(p_[:, :], bank.ap[:, :], AF.Exp, [bank.buf, cmask.buf], [p_.buf], bias=cmask[:, 0:1])
                    else:
                        K.actf(p_[:, :], bank.ap[:, :], AF.Exp, [bank.buf], [p_.buf])
                else:
                    s_ = sbt[j % 3]
                    bi = kb - nfull
                    K.op(dve, lambda: V.scalar_tensor_tensor(s_[:, :], dgT[:, bi * 512:(bi + 1) * 512], slope, bank.ap[:, :],
                                                             ALU.mult, ALU.add), [dgT.buf, bank.buf], [s_.buf])
                    K.actf(p_[:, :], s_[:, :], AF.Exp, [s_.buf], [p_.buf])

            def pvm(j):
                c, kb = items[j]
                p_ = ptt[j % NSB]
                K.mm(OT[c].ap[:, :], vT[i][:, kb, :], p_[:, :], kb == 0, kb == nkb - 1,
                     [vT[i].buf, p_.buf], [OT[c].buf], inc=False)
                K.mm(LT[c].ap[:, :], ONESB, p_[:, :], kb == 0, kb == nkb - 1,
                     [cb.buf, p_.buf], [LT[c].buf], inc=True)

            for j in range(n + LAG):
                if j < n:
                    qk(j)
                if 1 <= j <= n:
                    soft(j - 1)
                if j >= LAG:
                    pvm(j - LAG)
                if pend[0] is not None and j in (12, 14):
                    if next(pend[0], "done") == "done":
                        pend[0] = None
            if pend[0] is not None:
                for _ in pend[0]:
                    pass
                pend[0] = None
            K.op(dve, lambda: V.tensor_copy(ev["l0"][:, :], LT[0].ap[:, :]), [LT[0].buf], [ev["l0"].buf])
            K.op(dve, lambda: V.tensor_copy(ev["o0"][:, :], OT[0].ap[:, :]), [OT[0].buf], [ev["o0"].buf])
            K.op(dve, lambda: V.tensor_copy(ev["l1"][:, :], LT[1].ap[:, :]), [LT[1].buf], [ev["l1"].buf])
            K.op(dve, lambda: V.tensor_copy(ev["o1"][:, :], OT[1].ap[:, :]), [OT[1].buf], [ev["o1"].buf])
            pend[0] = epilogue(h, i, qi)
            next(pend[0])
            if qi == 3:
                for _ in pend[0]:
                    pass
                pend[0] = None

    pend = [None]

    def epilogue(h, i, qi):
        e = ep_
        K.op(dve, lambda: V.reciprocal(e["r0"][:, :], ev["l0"][:, :]), [ev["l0"].buf], [e["r0"].buf])
        K.op(dve, lambda: V.tensor_tensor(e["t0"][:, :], ev["o0"][:, :], e["r0"][:, :], ALU.mult),
             [ev["o0"].buf, e["r0"].buf], [e["t0"].buf])
        K.op(dve, lambda: V.reciprocal(e["r1"][:, :], ev["l1"][:, :]), [ev["l1"].buf], [e["r1"].buf])
        K.op(dve, lambda: V.tensor_tensor(e["t1"][:, :], ev["o1"][:, :], e["r1"][:, :], ALU.mult),
             [ev["o1"].buf, e["r1"].buf], [e["t1"].buf])
        K.op(dve, lambda: V.scalar_tensor_tensor(e["d"][:, :], e["t1"][:, :], NEGLAM, e["t0"][:, :], ALU.mult, ALU.add),
             [e["t1"].buf, e["t0"].buf, pv.buf], [e["d"].buf])
        yield
        K.actf(e["dsq"][:, :], e["d"][:, :], AF.Square, [e["d"].buf], [e["dsq"].buf])
        K.mm(MSB.ap[:, :], O128, e["dsq"][:, :], True, True, [cf.buf, e["dsq"].buf], [MSB.buf], inc=True)
        yield
        K.actf(e["dsq"][:, :], MSB.ap[:, :], AF.Ln, [MSB.buf], [e["dsq"].buf], bias=EPS)
        K.actf(e["rs"][:, :], e["dsq"][:, :], AF.Exp, [e["dsq"].buf], [e["rs"].buf], scale=-0.5)
        K.op(dve, lambda: V.scalar_tensor_tensor(e["y1"][:, :], e["d"][:, :], SUBLN, e["rs"][:, :], ALU.mult, ALU.mult),
             [e["d"].buf, e["rs"].buf, pv.buf], [e["y1"].buf])
        K.op(dve, lambda: V.tensor_tensor(ybT[h][:, qi * 512:(qi + 1) * 512], e["y1"][:, :],
                                          gT[i][:, qi * 512:(qi + 1) * 512], ALU.mult),
             [e["y1"].buf, gT[i].buf], [ybT[h].buf])

    att_load(0)
    for h in range(8):
        if h + 1 < 8:
            att_load(h + 1)
        att_head(h)

    if debug:
        dbg["ybT0"] = ybT[0]

    K.barrier()
    while len(K.stack) > n_att:
        K.stack.pop().__exit__(None, None, None)

    mixT = sb("mixT", [128, 16, TOK], BF16)
    yaT = [sb(f"yaT{h}", [128, TOK], BF16) for h in range(8)]
    for h in range(8):
        K.dma(sp, yaT[h][:, :], yas[h][:, :], [yas[h].buf], [yaT[h].buf])
    mbuf = [Buf(f"mx{n}") for n in range(4)]
    wab = [sb(f"wab{i}", [128, 2, 8, 128], BF16) for i in range(2)]
    gab = [sb(f"gab{i}", [128, 2, 512], BF16) for i in range(4)]
    m12 = [sb(f"m12{i}", [128, 2, 512]) for i in range(2)]
    wob = [sb(f"wob{i}", [128, 16, 512], BF16) for i in range(2)]
    xres = [sb(f"xres{i}", [128, 512]) for i in range(4)]
    osb = [sb(f"osb{i}", [128, 512]) for i in range(4)]
    def load_wo(cg):
        wg_ = wob[cg % 2]
        K.dma(pq, wg_[:, :, :], wo[:, cg * 512:(cg + 1) * 512].rearrange("(kc p) n -> p kc n", p=128), [], [wg_.buf])

    cnt = 0
    for j in range(16):
        if j in (4, 8):
            load_wo(j // 4 - 1)
        wj = wab[j % 2]
        K.dma(pq, wj[:, 0, :, :], wa[:, j * 128:(j + 1) * 128].rearrange("(kc p) n -> p kc n", p=128), [], [wj.buf])
        K.dma(pq, wj[:, 1, :, :], wb[:, j * 128:(j + 1) * 128].rearrange("(kc p) n -> p kc n", p=128), [], [wj.buf])
        for n in range(4):
            gj = gab[cnt % 4]
            mj = m12[cnt % 2]
            K.dma(K.aq, gj[:, 0, :], gs[j][:, n * 512:(n + 1) * 512], [gs[j].buf], [gj.buf])
            K.dma(K.aq, gj[:, 1, :], gs[16 + j][:, n * 512:(n + 1) * 512], [gs[16 + j].buf], [gj.buf])
            pa, pb_ = PS[(2 * cnt) % 4], PS[(2 * cnt + 1) % 4]
            for k in range(8):
                K.mm(pa.ap[:, :], wj[:, 0, k, :], yaT[k][:, n * 512:(n + 1) * 512], k == 0, k == 7,
                     [wj.buf, yaT[k].buf], [pa.buf], inc=(k == 7))
            for k in range(8):
                K.mm(pb_.ap[:, :], wj[:, 1, k, :], ybT[k][:, n * 512:(n + 1) * 512], k == 0, k == 7,
                     [wj.buf, ybT[k].buf], [pb_.buf], inc=(k == 7))
            K.op(dve, lambda: V.tensor_tensor(mj[:, 0, :], pa.ap[:, :], gj[:, 0, :], ALU.mult), [pa.buf, gj.buf], [mj.buf])
            K.op(dve, lambda: V.tensor_tensor(mj[:, 1, :], pb_.ap[:, :], gj[:, 1, :], ALU.mult), [pb_.buf, gj.buf], [mj.buf])
            K.op(dve, lambda: V.tensor_tensor(mixT[:, j, n * 512:(n + 1) * 512], mj[:, 0, :], mj[:, 1, :], ALU.add),
                 [mj.buf], [mbuf[n]])
            cnt += 1
    out_toks = []
    cnt = 0
    for cg in range(4):
        wg = wob[cg % 2]
        if cg >= 2:
            load_wo(cg)
        for t in range(16):
            xr = xres[cnt % 4]
            ob_ = osb[cnt % 4]
            bank = PS[4 + cnt % 4]
            K.dma(K.aq, xr[:, :], xo[t * 128:(t + 1) * 128, cg * 512:(cg + 1) * 512], [], [xr.buf])
            for k in range(16):
                K.mm(bank.ap[:, :], mixT[:, k, t * 128:(t + 1) * 128], wg[:, k, :], k == 0, k == 15,
                     [mbuf[t // 4], wg.buf], [bank.buf], inc=(k == 15))
            K.op(dve, lambda: V.tensor_tensor(ob_[:, :], bank.ap[:, :], xr[:, :], ALU.add), [bank.buf, xr.buf], [ob_.buf])
            out_toks.append(K.dma(sp, y[t * 128:(t + 1) * 128, cg * 512:(cg + 1) * 512], ob_[:, :], [ob_.buf], []))
            cnt += 1

    dbg_out = {}
    if debug:
        for name, t_ in dbg.items():
            src_ = t_[:, :]
            o = nc.dram_tensor("dbg_" + name, list(src_.shape), src_.dtype, kind="ExternalOutput").ap()
            out_toks.append(K.dma(sp, o, src_, [t_.buf], []))
            dbg_out[name] = "dbg_" + name
    for t_ in K.last_tokens():
        if t_.eng.dma_sems is not None:
            sp.wait(t_)
    K.close()
    return nc, dbg_out


def _prep(inputs):
    f = lambda a: np.ascontiguousarray(np.asarray(a, dtype=np.float32))
    x = f(inputs["x"])
    w_in = f(inputs["w_in"])[0]
    perm = []
    for h in range(8):
        for blk in range(4):
            perm.extend(range(blk * 1024 + h * 128, blk * 1024 + (h + 1) * 128))
    perm.extend(range(4096, NIN))
    w_perm = np.ascontiguousarray(w_in[:, np.array(perm)])
    cf, cb, kpos, qpos, dg = _consts()
    shared = {
        "w_in": w_perm,
        "wa": f(inputs["w_branch_a"])[0], "wb": f(inputs["w_branch_b"])[0], "wo": f(inputs["w_out"])[0],
        "norm_w": f(inputs["norm_w"])[0], "alb": f(inputs["a_lower_bound"]), "aon": f(inputs["a_out_norm"])[0],
        "bqn": f(inputs["b_q_norm"])[0], "bkn": f(inputs["b_k_norm"])[0],
        "blam": f(inputs["b_lambda"])[0].reshape(256), "bsub": f(inputs["b_subln"])[0],
        "c_f": cf, "c_b": cb, "c_kpos": kpos, "c_qpos": qpos.reshape(32, TOK), "c_dg": dg,
    }
    zeros = np.zeros((TOK, D), np.float32)
    in_maps = []
    for c in range(NCORES):
        b, s = c // 2, c % 2
        m = dict(shared)
        m["xo"] = np.ascontiguousarray(x[b, s * TOK:(s + 1) * TOK])
        m["xp"] = np.ascontiguousarray(x[b, 0:TOK]) if s == 1 else zeros
        m["c_mask"] = np.full((128, 1), 0.0 if s == 1 else -1.0e9, np.float32)
        in_maps.append(m)
    return in_maps


def kernel(**inputs):
    in_maps = _prep(inputs)
    nc, _ = build(debug=False)
    res = run_bass_kernel_spmd(nc, in_maps, core_ids=list(range(NCORES)))
    out = np.empty((4, 2 * TOK, D), np.float32)
    for c in range(NCORES):
        b, s = c // 2, c % 2
        out[b, s * TOK:(s + 1) * TOK] = res.results[c]["y"]
    return out
```

```python
import numpy as np
import concourse.bass as bass
import concourse.mybir as mybir
from concourse.bass_utils import run_bass_kernel_spmd

F32 = mybir.dt.float32
BF16 = mybir.dt.bfloat16
AF = mybir.ActivationFunctionType
ALU = mybir.AluOpType
AX = mybir.AxisListType

D = 2048
TOK = 2048
NIN = 12288
EPS = 1e-6
NCORES = 8
LAMBDA_INIT = 0.2
NDMA_SEMS = 8

C_A = 0
C_BQ = 4096
C_BK = 5120
C_BV = 6144
C_BG = 7168
C_GA = 8192
C_GB = 10240


class Tok:
    __slots__ = ("sem", "val", "eng")

    def __init__(self, sem, val, eng):
        self.sem, self.val, self.eng = sem, val, eng


class Buf:
    __slots__ = ("name", "w", "r")

    def __init__(self, name=""):
        self.name, self.w, self.r = name, None, {}


class T:
    def __init__(self, ap, name=""):
        self.ap = ap
        self.buf = Buf(name)

    def __getitem__(self, k):
        return self.ap[k]


class Eng:
    def __init__(self, name, h, sem=None, dma_sems=None, seen=None):
        self.name, self.h, self.sem = name, h, sem
        self.count = 0
        self.dma_sems = dma_sems
        self.n = 0
        self.seen = {} if seen is None else seen
        self.pending = []

    def wait(self, tok):
        assert tok.val is not None, "waiting on an unresolved PE token"
        if self.seen.get(tok.sem, 0) >= tok.val:
            return
        self.h.wait_ge(tok.sem, tok.val)
        self.seen[tok.sem] = tok.val


class Kern:
    def __init__(self, nc):
        self.nc = nc
        self.stack = []
        mk = lambda n: self.enter(nc.semaphore(n))
        self.pe = Eng("pe", nc.tensor, mk("s_pe"))
        self.act = Eng("act", nc.scalar, mk("s_act"))
        self.dve = Eng("dve", nc.vector, mk("s_dve"))
        self.sp = Eng("sp", nc.sync, dma_sems=[mk(f"s_sp{i}") for i in range(NDMA_SEMS)])
        self.pq = Eng("pq", nc.gpsimd, dma_sems=[mk(f"s_pq{i}") for i in range(NDMA_SEMS)])
        self.aq = Eng("aq", nc.scalar, dma_sems=[mk(f"s_aq{i}") for i in range(NDMA_SEMS)], seen=self.act.seen)
        self.engines = [self.pe, self.act, self.dve, self.sp, self.pq, self.aq]

    def enter(self, cm):
        v = cm.__enter__()
        self.stack.append(cm)
        return v

    def close(self):
        while self.stack:
            self.stack.pop().__exit__(None, None, None)

    def op(self, eng, fn, reads=(), writes=(), inc=True):
        is_dma = eng.dma_sems is not None
        for b in reads:
            w = b.w
            if w is not None and not (w.eng is eng and eng.name == "pe"):
                eng.wait(w)
        for b in writes:
            w = b.w
            if w is not None and (is_dma or w.eng is not eng):
                eng.wait(w)
            for t in b.r.values():
                if is_dma or t.eng is not eng:
                    eng.wait(t)
        if is_dma:
            slot = eng.n % NDMA_SEMS
            rnd = eng.n // NDMA_SEMS
            sem = eng.dma_sems[slot]
            if rnd > 0 and eng.seen.get(sem, 0) < 16 * rnd:
                eng.h.wait_ge(sem, 16 * rnd)
                eng.seen[sem] = 16 * rnd
            ins = fn()
            ins.then_inc(sem, 16)
            tok = Tok(sem, 16 * (rnd + 1), eng)
            eng.n += 1
        else:
            ins = fn()
            if inc:
                eng.count += 1
                ins.then_inc(eng.sem, 1)
                tok = Tok(eng.sem, eng.count, eng)
                for p in eng.pending:
                    p.val = eng.count
                eng.pending = []
            else:
                tok = Tok(eng.sem, None, eng)
                eng.pending.append(tok)
        for b in reads:
            b.r[(tok.sem, id(eng))] = tok
        for b in writes:
            b.w = tok
            b.r = {}
        return tok

    def last_tokens(self):
        toks = []
        for e in self.engines:
            if e.dma_sems is None:
                assert not e.pending
                if e.count:
                    toks.append(Tok(e.sem, e.count, e))
            else:
                for i, sem in enumerate(e.dma_sems):
                    k = (e.n - i + NDMA_SEMS - 1) // NDMA_SEMS
                    if k > 0:
                        toks.append(Tok(sem, 16 * k, e))
        return toks

    def barrier(self):
        toks = self.last_tokens()
        for e in self.engines:
            for t in toks:
                if t.eng is e and e.dma_sems is None:
                    continue
                e.wait(t)

    def mm(self, out, lhsT, rhs, start, stop, reads, writes, inc):
        return self.op(self.pe, lambda: self.nc.tensor.matmul(out, lhsT, rhs, start=start, stop=stop),
                       reads, writes, inc)

    def tr(self, out, in_, ident, reads, writes, inc=True):
        return self.op(self.pe, lambda: self.nc.tensor.transpose(out, in_, ident), reads, writes, inc)

    def actf(self, out, in_, func, reads, writes, bias=0.0, scale=1.0, accum_out=None):
        def f():
            kw = {}
            if accum_out is not None:
                kw["accum_out"] = accum_out
            return self.nc.scalar.activation(out, in_, func, bias=bias, scale=scale, **kw)
        return self.op(self.act, f, reads, writes)

    def dma(self, eng, out, in_, reads, writes):
        return self.op(eng, lambda: eng.h.dma_start(out=out, in_=in_), reads, writes)


def _consts():
    i = np.arange(128)[:, None]
    j = np.arange(128)[None, :]
    trip = (i <= j).astype(np.float32) - (i <= 63).astype(np.float32)
    triu = (i <= j).astype(np.float32)
    bd = ((i // 64) == (j // 64)).astype(np.float32) / 64.0
    o128 = np.full((128, 128), 1.0 / 128.0, np.float32)
    sel = np.concatenate([(i <= 63), (i >= 64)], axis=1).astype(np.float32)
    cf = np.concatenate([trip, triu, bd, o128, sel], axis=1).astype(np.float32)
    cb = np.concatenate([np.eye(128, dtype=np.float32), np.ones((128, 128), np.float32), trip, sel], axis=1)
    tk = np.arange(4096)
    kpos = np.stack([64.0 * (tk // 64), (tk % 64).astype(np.float64), np.ones(4096), np.ones(4096)]).astype(np.float32)
    tq = TOK + np.arange(TOK)
    qpos = np.zeros((8, 4, TOK), np.float32)
    for h in range(8):
        sl = 2.0 ** (-(h + 1))
        qpos[h] = np.stack([np.full(TOK, sl), np.full(TOK, sl), -sl * 64.0 * (tq // 64), -sl * (tq % 64)])
    jq = np.arange(512)[None, :]
    dg = []
    for bi in range(4):
        kp = 128 * bi + i
        allowed = (kp // 64) <= (jq // 64)
        fix = np.where(kp > jq, -2.0 * (kp - jq), 0.0)
        dg.append(np.where(allowed, fix, -1.0e9).astype(np.float32))
    dg = np.concatenate(dg, axis=1)
    return cf, cb, kpos, qpos, dg


def build(debug=False):
    nc = bass.Bass("TRN2", target_bir_lowering=False)
    K = Kern(nc)
    pe, act, dve, sp, pq = K.pe, K.act, K.dve, K.sp, K.pq
    V = nc.vector
    dt_in = lambda n, s: nc.dram_tensor(n, s, F32, kind="ExternalInput").ap()
    xo = dt_in("xo", [TOK, D])
    xp = dt_in("xp", [TOK, D])
    w_in = dt_in("w_in", [D, NIN])
    wa = dt_in("wa", [1024, D])
    wb = dt_in("wb", [1024, D])
    wo = dt_in("wo", [D, D])
    norm_w = dt_in("norm_w", [D])
    alb = dt_in("alb", [2, 1024])
    aon = dt_in("aon", [128])
    bqn = dt_in("bqn", [64])
    bkn = dt_in("bkn", [64])
    blam = dt_in("blam", [256])
    bsub = dt_in("bsub", [128])
    c_f = dt_in("c_f", [128, 514])
    c_b = dt_in("c_b", [128, 386])
    c_kpos = dt_in("c_kpos", [4, 4096])
    c_qpos = dt_in("c_qpos", [32, TOK])
    c_dg = dt_in("c_dg", [128, 2048])
    c_mask = dt_in("c_mask", [128, 1])
    y = nc.dram_tensor("y", [TOK, D], F32, kind="ExternalOutput").ap()
    scr = lambda n, s: T(nc.dram_tensor(n, s, BF16, kind="Internal").ap(), n)
    kc = [scr(f"kc{h}", [128, 2 * TOK]) for h in range(8)]
    vc = scr("vc", [2 * TOK, 1024])
    qb = [scr(f"qb{h}", [128, TOK]) for h in range(8)]
    gb = [scr(f"gb{h}", [128, TOK]) for h in range(8)]
    gs = [scr(f"gs{j}", [128, TOK]) for j in range(32)]
    yas = [scr(f"yas{h}", [128, TOK]) for h in range(8)]
    dbg = {}

    def sb(name, shape, dt=F32):
        return T(K.enter(nc.sbuf_tensor(name, shape, dt)), name)

    PS = [T(K.enter(nc.psum_tensor(f"ps{i}", [128, 512], F32)), f"ps{i}") for i in range(8)]

    cf = sb("cf", [128, 514])
    cb = sb("cb", [128, 386], BF16)
    cmask = sb("cmask", [128, 1])
    K.dma(sp, cf[:, :], c_f[:, :], [], [cf.buf])
    K.dma(pq, cb[:, :], c_b[:, :], [], [cb.buf])
    K.dma(sp, cmask[:, :], c_mask[:, :], [], [cmask.buf])
    TRIP, TRIU, BD, O128, SEL = (cf[:, 0:128], cf[:, 128:256], cf[:, 256:384], cf[:, 384:512], cf[:, 512:514])
    IDENT, ONESB, TRIPB, SELB = cb[:, 0:128], cb[:, 128:256], cb[:, 256:384], cb[:, 384:386]

    gainA = sb("gainA", [128, 128])
    K.dma(sp, gainA[:, :], aon.partition_broadcast(128), [], [gainA.buf])
    pv = sb("pv", [128, 16])
    pvr = sb("pvr", [128, 4])
    with nc.allow_non_contiguous_dma(reason="tiny parameter vectors"):
        for half in range(2):
            K.dma(sp, pvr[64 * half:64 * half + 64, 0:1], bqn.rearrange("(p o) -> p o", o=1), [], [pvr.buf])
            K.dma(sp, pvr[64 * half:64 * half + 64, 1:2], bkn.rearrange("(p o) -> p o", o=1), [], [pvr.buf])
        K.dma(sp, pvr[:, 2:3], bsub.rearrange("(p o) -> p o", o=1), [], [pvr.buf])
    lam4 = sb("lam4", [128, 256])
    K.dma(sp, lam4[:, :], blam.partition_broadcast(128), [], [lam4.buf])
    lamt = sb("lamt", [128, 128])
    K.op(dve, lambda: V.tensor_scalar(pv[:, 0:1], pvr[:, 0:1], 0.125, None, ALU.mult), [pvr.buf], [pv.buf])
    K.op(dve, lambda: V.tensor_copy(pv[:, 1:2], pvr[:, 1:2]), [pvr.buf], [pv.buf])
    K.op(dve, lambda: V.tensor_scalar(pv[:, 2:3], pvr[:, 2:3], 1.0 - LAMBDA_INIT, None, ALU.mult), [pvr.buf], [pv.buf])
    K.op(dve, lambda: V.tensor_tensor(lamt[:, 0:64], lam4[:, 0:64], lam4[:, 64:128], ALU.mult), [lam4.buf], [lamt.buf])
    K.op(dve, lambda: V.tensor_tensor(lamt[:, 64:128], lam4[:, 128:192], lam4[:, 192:256], ALU.mult), [lam4.buf], [lamt.buf])
    K.op(dve, lambda: V.reduce_sum(pv[:, 4:5], lamt[:, 0:64], axis=AX.X), [lamt.buf], [pv.buf])
    K.op(dve, lambda: V.reduce_sum(pv[:, 5:6], lamt[:, 64:128], axis=AX.X), [lamt.buf], [pv.buf])
    K.actf(pv[:, 6:8], pv[:, 4:6], AF.Exp, [pv.buf], [pv.buf])
    K.op(dve, lambda: V.tensor_tensor(pv[:, 8:9], pv[:, 7:8], pv[:, 6:7], ALU.subtract), [pv.buf], [pv.buf])
    K.op(dve, lambda: V.tensor_scalar(pv[:, 3:4], pv[:, 8:9], -LAMBDA_INIT, None, ALU.add), [pv.buf], [pv.buf])
    GQ, GK, SUBLN, NEGLAM = pv[:, 0:1], pv[:, 1:2], pv[:, 2:3], pv[:, 3:4]


    n_proj = len(K.stack)
    hT = sb("hT", [128, 16, TOK], BF16)
    hbuf = [[Buf(f"h{t}_{hf}") for hf in range(2)] for t in range(16)]
    nwB = sb("nwB", [128, D])
    K.dma(sp, nwB[:, :], norm_w.partition_broadcast(128), [], [nwB.buf])
    omlB = sb("omlB", [128, 1024])

    Wt = [sb(f"Wt{i}", [128, 16, 512], BF16) for i in range(2)]
    wcnt = [0]
    xt = [sb(f"xt{i}", [128, D]) for i in range(3)]
    xn = sb("xn", [128, D], BF16)
    junk = xn
    a01 = xt[0]
    K.dma(sp, a01[:, 0:1024], alb[0, :].partition_broadcast(128), [], [a01.buf])
    K.dma(sp, a01[:, 1024:2048], alb[1, :].partition_broadcast(128), [], [a01.buf])
    K.op(dve, lambda: V.tensor_tensor(a01[:, 0:1024], a01[:, 1024:2048], a01[:, 0:1024], ALU.subtract), [a01.buf], [a01.buf])
    K.actf(omlB[:, :], a01[:, 0:1024], AF.Sigmoid, [a01.buf], [omlB.buf])
    K.op(dve, lambda: V.tensor_scalar(omlB[:, :], omlB[:, :], -0.5, None, ALU.mult), [omlB.buf], [omlB.buf])
    st1 = sb("st1", [128, 8])
    Sst = [sb(f"S{h}", [128, 128]) for h in range(8)]
    stg = [sb(f"stg{i}", [128, 512], BF16) for i in range(4)]
    stgc = [0]
    nsq = [sb(f"nsq{i}", [128, 512]) for i in range(2)]
    nrs = [sb(f"nrs{i}", [128, 512]) for i in range(2)]
    ncnt = [0]
    raw = [sb(f"raw{i}", [128, 4, 512]) for i in range(2)]
    vbq = [sb(f"vbq{i}", [128, 4, 128], BF16) for i in range(2)]
    hq = {n: sb("hq_" + n, shp, dt) for n, shp, dt in [
        ("lf", [128, 4, 128], F32), ("lfh", [128, 4, 128], BF16), ("lfl", [128, 4, 128], BF16), ("ep", [128, 4, 128], F32), ("en", [128, 4, 128], F32),
        ("qt", [128, 4, 128], BF16), ("kt", [128, 4, 128], BF16), ("qkT", [128, 1024], BF16),
        ("pm", [128, 4, 128], BF16), ("ug", [128, 4, 128], F32), ("sa", [128, 5, 128], F32),
        ("sp", [128, 4, 128], BF16), ("osq", [128, 4, 128], F32), ("ya", [128, 4, 128], BF16),
        ("ebc", [128, 4, 2], F32), ("gc", [128, 4], F32), ("ss", [128, 4], F32), ("lnv", [128, 4], F32),
        ("rstd", [128, 4], F32), ("yat", [128, 512], BF16), ("gg", [128, 4, 128], F32)]}

    def build_front(t):
        if True:
            x_t = xt[t % 3]
            K.actf(junk[:, :], x_t[:, :], AF.Square, [x_t.buf], [xn.buf, st1.buf], accum_out=st1[:, 0:1])
            K.actf(st1[:, 1:2], st1[:, 0:1], AF.Sqrt, [st1.buf], [st1.buf], bias=EPS, scale=1.0 / D)
            K.op(dve, lambda: V.reciprocal(st1[:, 2:3], st1[:, 1:2]), [st1.buf], [st1.buf])
            K.op(dve, lambda: V.scalar_tensor_tensor(xn[:, :], x_t[:, :], st1[:, 2:3], nwB[:, :], ALU.mult, ALU.mult),
                 [x_t.buf, st1.buf, nwB.buf], [xn.buf])

    def build_back(t):
        if True:
            for half in range(2):
                bank = PS[half]
                pb = bank.ap[:, :].bitcast(BF16)
                for c in range(8):
                    kcix = half * 8 + c
                    K.tr(pb[:, c * 128:(c + 1) * 128], xn[:, kcix * 128:(kcix + 1) * 128], IDENT,
                         [xn.buf, cb.buf], [bank.buf], inc=(c == 7))
                src = pb.rearrange("p (a b) -> p a b", b=128)
                dst = hT[:, half * 8:half * 8 + 8, t * 128:(t + 1) * 128]
                if half == 0:
                    K.op(dve, lambda: V.tensor_copy(dst, src), [bank.buf], [hbuf[t][0]])
                else:
                    K.op(act, lambda: nc.scalar.copy(dst, src), [bank.buf], [hbuf[t][1]])

    def load_w(c0, ncols):
        w = Wt[wcnt[0] % 2]
        wcnt[0] += 1
        src = w_in[:, c0:c0 + ncols].rearrange("(kc p) n -> p kc n", p=128)
        K.dma(pq, w[:, :, 0:ncols], src, [], [w.buf])
        return w

    def gemm_fm(w, col, n, bank):
        for k in range(16):
            K.mm(bank.ap[:, :], w[:, k, col:col + 128], hT[:, k, n * 512:(n + 1) * 512], k == 0, k == 15,
                 [w.buf] + sum(hbuf[4 * n:4 * n + 4], []), [bank.buf], inc=(k == 15))

    def gemm_tm(w, col, ncols, t, bank):
        for k in range(16):
            K.mm(bank.ap[:, 0:ncols], hT[:, k, t * 128:(t + 1) * 128], w[:, k, col:col + ncols], k == 0, k == 15,
                 [w.buf] + hbuf[t], [bank.buf], inc=(k == 15))

    def next_stg():
        s = stg[stgc[0] % 4]
        stgc[0] += 1
        return s

    def qknorm(bank, gain, dst_dram, dst_ap):
        i = ncnt[0] % 2
        ncnt[0] += 1
        sq, rs = nsq[i], nrs[i]
        K.actf(sq[:, :], bank.ap[:, :], AF.Square, [bank.buf], [sq.buf])
        msb = PS[4]
        K.mm(msb.ap[:, :], BD, sq[:, :], True, True, [cf.buf, sq.buf], [msb.buf], inc=True)
        K.actf(rs[:, :], msb.ap[:, :], AF.Sqrt, [msb.buf], [rs.buf], bias=EPS)
        K.op(dve, lambda: V.reciprocal(rs[:, :], rs[:, :]), [rs.buf], [rs.buf])
        s = next_stg()
        K.op(dve, lambda: V.scalar_tensor_tensor(s[:, :], bank.ap[:, :], gain, rs[:, :], ALU.mult, ALU.mult),
             [bank.buf, rs.buf, pv.buf], [s.buf])
        K.dma(sp, dst_ap, s[:, :], [s.buf], [dst_dram.buf])

    gcnt = [0]

    def next_bank():
        b = PS[2 + gcnt[0] % 2]
        gcnt[0] += 1
        return b

    def proj_k(tok_off):
        for g in range(2):
            w = load_w(C_BK + 512 * g, 512)
            for m in range(4):
                h = 4 * g + m
                for n in range(4):
                    bank = next_bank()
                    gemm_fm(w, 128 * m, n, bank)
                    qknorm(bank, GK, kc[h], kc[h][:, tok_off + n * 512: tok_off + (n + 1) * 512])

    def proj_q():
        for g in range(2):
            w = load_w(C_BQ + 512 * g, 512)
            for m in range(4):
                h = 4 * g + m
                for n in range(4):
                    bank = next_bank()
                    gemm_fm(w, 128 * m, n, bank)
                    qknorm(bank, GQ, qb[h], qb[h][:, n * 512:(n + 1) * 512])

    def proj_act_fm(c0, func, dsts):
        ngroups = len(dsts) // 4
        for g in range(ngroups):
            w = load_w(c0 + 512 * g, 512)
            for m in range(4):
                for n in range(4):
                    bank = next_bank()
                    gemm_fm(w, 128 * m, n, bank)
                    s = next_stg()
                    K.actf(s[:, :], bank.ap[:, :], func, [bank.buf], [s.buf])
                    d = dsts[4 * g + m]
                    K.dma(sp, d[:, n * 512:(n + 1) * 512], s[:, :], [s.buf], [d.buf])

    def build_and_v(xsrc, tok_off):
        ws = [load_w(C_BV + 512 * g, 512) for g in range(2)]
        def load_x(t):
            K.dma(sp, xt[t % 3][:, :], xsrc[t * 128:(t + 1) * 128, :], [], [xt[t % 3].buf])
        load_x(0)
        load_x(1)
        load_x(2)
        build_front(0)
        build_back(0)
        for t in range(16):
            if t + 1 < 16:
                build_front(t + 1)
                if t + 3 < 16:
                    load_x(t + 3)
            for g in range(2):
                w = ws[g]
                bank = next_bank()
                gemm_tm(w, 0, 512, t, bank)
                s = next_stg()
                K.op(dve, lambda: V.tensor_copy(s[:, :], bank.ap[:, :]), [bank.buf], [s.buf])
                K.dma(sp, vc[tok_off + t * 128: tok_off + (t + 1) * 128, g * 512:(g + 1) * 512], s[:, :],
                      [s.buf], [vc.buf])
            if t + 1 < 16:
                build_back(t + 1)

    GB = [PS[0], PS[1], PS[2], PS[3]]
    X1, X2, X3, X4 = PS[4], PS[5], PS[6], PS[7]

    def bc4(ap2d):
        return ap2d.unsqueeze(1).broadcast_to([128, 4, 128])

    def pass1(P, vb, own):
        zc, ic = (1, 2) if own else (0, 1)
        z = P[:, :, zc * 128:(zc + 1) * 128]
        K.actf(z, z, AF.Tanh, [P.buf], [P.buf], scale=0.5)
        if own:
            q = P[:, :, 0:128]
            gg = P[:, :, 384:512]
            K.actf(q, q, AF.Silu, [P.buf], [P.buf])
            K.actf(gg, gg, AF.Silu, [P.buf], [P.buf])
        K.actf(vb[:, :, :], P[:, :, ic * 128:(ic + 1) * 128], AF.Copy, [P.buf], [vb.buf])

    def pass2(h, qd, P, vb, own):
        g = hq
        S = Sst[h]
        zc = 1 if own else 0
        kk = P[:, :, zc * 128:(zc + 1) * 128]
        K.op(dve, lambda: V.scalar_tensor_tensor(kk, kk, 1.0, bc4(omlB[:, h * 128:(h + 1) * 128]), ALU.subtract, ALU.mult),
             [P.buf, omlB.buf], [P.buf])
        K.actf(g["lf"][:, :, :], kk, AF.Ln, [P.buf], [g["lf"].buf], bias=1.0, scale=-1.0)
        K.op(dve, lambda: V.tensor_copy(g["lfh"][:, :, :], g["lf"][:, :, :]), [g["lf"].buf], [g["lfh"].buf])
        K.op(dve, lambda: V.tensor_tensor(g["lfl"][:, :, :], g["lf"][:, :, :], g["lfh"][:, :, :], ALU.subtract),
             [g["lf"].buf, g["lfh"].buf], [g["lfl"].buf])
        yield
        for t in range(4):
            K.mm(X1.ap[:, t * 128:(t + 1) * 128], TRIPB, g["lfh"][:, t, :], True, False, [cb.buf, g["lfh"].buf], [X1.buf], inc=False)
            K.mm(X1.ap[:, t * 128:(t + 1) * 128], TRIPB, g["lfl"][:, t, :], False, True, [cb.buf, g["lfl"].buf], [X1.buf], inc=(t == 3))
        for t in range(4):
            K.mm(X2.ap[:, 2 * t:2 * t + 2], g["lfh"][:, t, :], SELB, True, False, [cb.buf, g["lfh"].buf], [X2.buf], inc=False)
            K.mm(X2.ap[:, 2 * t:2 * t + 2], g["lfl"][:, t, :], SELB, False, True, [cb.buf, g["lfl"].buf], [X2.buf], inc=(t == 3))
        yield
        x1v = X1.ap[:, :].rearrange("p (a b) -> p a b", b=128)
        K.actf(g["en"][:, :, :], x1v, AF.Exp, [X1.buf], [g["en"].buf], scale=-1.0)
        K.actf(g["ebc"][:, :, :], X2.ap[:, 0:8].rearrange("p (a b) -> p a b", b=2), AF.Exp, [X2.buf], [g["ebc"].buf])
        if own:
            K.actf(g["ep"][:, :, :], x1v, AF.Exp, [X1.buf], [g["ep"].buf])
        K.op(dve, lambda: V.tensor_tensor(g["kt"][:, :, :], kk, g["en"][:, :, :], ALU.mult), [P.buf, g["en"].buf], [g["kt"].buf])
        if own:
            K.op(dve, lambda: V.tensor_tensor(g["qt"][:, :, :], P[:, :, 0:128], g["ep"][:, :, :], ALU.mult),
                 [P.buf, g["ep"].buf], [g["qt"].buf])
            K.op(dve, lambda: V.tensor_tensor(g["gg"][:, :, :], P[:, :, 384:512], bc4(gainA[:, :]), ALU.mult),
                 [P.buf, gainA.buf], [g["gg"].buf])
        K.op(dve, lambda: V.tensor_tensor(g["gc"][:, 0:3], g["ebc"][:, 1:4, 0], g["ebc"][:, 0:3, 1], ALU.mult),
             [g["ebc"].buf], [g["gc"].buf])
        K.op(dve, lambda: V.tensor_copy(g["gc"][:, 3:4], g["ebc"][:, 3, 1:2]), [g["ebc"].buf], [g["gc"].buf])
        K.op(dve, lambda: V.tensor_scalar(g["sa"][:, 0, :], S[:, :], g["ebc"][:, 0, 0:1], None, ALU.mult),
             [S.buf, g["ebc"].buf], [g["sa"].buf])
        yield
        for t in range(4):
            K.mm(X3.ap[:, t * 128:(t + 1) * 128], g["kt"][:, t, :], vb[:, t, :], True, True, [g["kt"].buf, vb.buf], [X3.buf], inc=(t == 3))
        if own:
            x2b = X2.ap[:, :].bitcast(BF16)
            for t in range(4):
                K.tr(x2b[:, t * 128:(t + 1) * 128], g["qt"][:, t, :], IDENT, [g["qt"].buf, cb.buf], [X2.buf], inc=False)
            for t in range(4):
                K.tr(x2b[:, 512 + t * 128:512 + (t + 1) * 128], g["kt"][:, t, :], IDENT, [g["kt"].buf, cb.buf], [X2.buf], inc=(t == 3))
        yield
        K.op(dve, lambda: V.tensor_tensor(g["ug"][:, :, :], X3.ap[:, :].rearrange("p (a b) -> p a b", b=128),
                                          g["gc"][:, 0:4].unsqueeze(2).broadcast_to([128, 4, 128]), ALU.mult),
             [X3.buf, g["gc"].buf], [g["ug"].buf])
        if own:
            K.actf(g["qkT"][:, :], x2b[:, :], AF.Copy, [X2.buf], [g["qkT"].buf])
        yield
        for t in range(4):
            dst = g["sa"][:, t + 1, :] if t < 3 else S[:, :]
            dbuf = g["sa"].buf if t < 3 else S.buf
            K.op(dve, lambda: V.scalar_tensor_tensor(dst, g["sa"][:, t, :], g["gc"][:, t:t + 1], g["ug"][:, t, :], ALU.mult, ALU.add),
                 [g["sa"].buf, g["gc"].buf, g["ug"].buf], [dbuf])
        if not own:
            return
        K.actf(g["sp"][:, :, :], g["sa"][:, 0:4, :], AF.Copy, [g["sa"].buf], [g["sp"].buf])
        for t in range(4):
            K.mm(X1.ap[:, t * 128:(t + 1) * 128], g["qkT"][:, 512 + t * 128:512 + (t + 1) * 128], g["qkT"][:, t * 128:(t + 1) * 128],
                 True, True, [g["qkT"].buf], [X1.buf], inc=(t == 3))
        yield
        K.op(dve, lambda: V.tensor_tensor(g["pm"][:, :, :], x1v, bc4(TRIU), ALU.mult), [X1.buf, cf.buf], [g["pm"].buf])
        yield
        for t in range(4):
            K.mm(X4.ap[:, t * 128:(t + 1) * 128], g["pm"][:, t, :], vb[:, t, :], True, False, [g["pm"].buf, vb.buf], [X4.buf], inc=False)
            K.mm(X4.ap[:, t * 128:(t + 1) * 128], g["qkT"][:, t * 128:(t + 1) * 128], g["sp"][:, t, :], False, True,
                 [g["qkT"].buf, g["sp"].buf], [X4.buf], inc=(t == 3))
        yield
        x4v = X4.ap[:, :].rearrange("p (a b) -> p a b", b=128)
        K.actf(g["osq"][:, :, :], x4v, AF.Square, [X4.buf], [g["osq"].buf])
        K.op(dve, lambda: V.reduce_sum(g["ss"][:, :], g["osq"][:, :, :], axis=AX.X), [g["osq"].buf], [g["ss"].buf])
        K.actf(g["lnv"][:, :], g["ss"][:, :], AF.Ln, [g["ss"].buf], [g["lnv"].buf], bias=EPS, scale=1.0 / 128.0)
        K.actf(g["rstd"][:, :], g["lnv"][:, :], AF.Exp, [g["lnv"].buf], [g["rstd"].buf], scale=-0.5)
        for t in range(4):
            K.op(dve, lambda: V.scalar_tensor_tensor(g["ya"][:, t, :], X4.ap[:, t * 128:(t + 1) * 128], g["rstd"][:, t:t + 1],
                                                     g["gg"][:, t, :], ALU.mult, ALU.mult),
                 [X4.buf, g["rstd"].buf, g["gg"].buf], [g["ya"].buf])
        yield
        x3b = X3.ap[:, :].bitcast(BF16)
        for t in range(4):
            K.tr(x3b[:, t * 128:(t + 1) * 128], g["ya"][:, t, :], IDENT, [g["ya"].buf, cb.buf], [X3.buf], inc=(t == 3))
        yield
        K.actf(g["yat"][:, :], x3b[:, 0:512], AF.Copy, [X3.buf], [g["yat"].buf])
        K.dma(sp, yas[h][:, qd * 512:(qd + 1) * 512], g["yat"][:, :], [g["yat"].buf], [yas[h].buf])

    def proj_a(own):
        ncols = 512 if own else 256
        sched = [1, 2, 2, 2] if own else [1, 2, 2, 1]
        tsched = [2, 0, 0, 0]
        cur, tail = None, None
        qcount = 0

        def adv(gen, k):
            if gen is None:
                return None
            for _ in range(k):
                if next(gen, "done") == "done":
                    return None
            return gen

        for h in range(8):
            w = load_w(C_A + 512 * h + (0 if own else 128), ncols)
            for qd in range(4):
                P, vb = raw[qcount % 2], vbq[qcount % 2]
                qcount += 1
                for tt in range(4):
                    t = 4 * qd + tt
                    bank = GB[tt]
                    gemm_tm(w, 0, ncols, t, bank)
                    K.op(dve, lambda: V.tensor_copy(P[:, tt, 0:ncols], bank.ap[:, 0:ncols]), [bank.buf], [P.buf])
                    tail = adv(tail, tsched[tt])
                    if tt == 1 and tail is not None:
                        for _ in tail:
                            pass
                        tail = None
                    cur = adv(cur, sched[tt])
                if own:
                    cur = adv(cur, 2)
                tail = cur
                pass1(P, vb, own)
                cur = pass2(h, qd, P, vb, own)
        for gen in (tail, cur):
            if gen is not None:
                for _ in gen:
                    pass

    for h in range(8):
        K.op(dve, lambda: V.memset(Sst[h][:, :], 0.0), [], [Sst[h].buf])

    build_and_v(xp, 0)
    proj_k(0)
    proj_a(False)
    build_and_v(xo, TOK)
    proj_k(TOK)
    proj_q()
    proj_act_fm(C_BG, AF.Silu, gb)
    proj_act_fm(C_GA, AF.Sigmoid, gs)
    proj_a(True)


    K.barrier()
    while len(K.stack) > n_proj:
        K.stack.pop().__exit__(None, None, None)

    ybT = [sb(f"ybT{h}", [128, TOK], BF16) for h in range(8)]
    n_att = len(K.stack)
    dgT = sb("dgT", [128, 2048])
    K.dma(sp, dgT[:, :], c_dg[:, :], [], [dgT.buf])
    kTm = [[sb(f"kT{c}_{i}", [128, 2 * TOK], BF16) for i in range(2)] for c in range(2)]
    qTm = [[sb(f"qT{c}_{i}", [128, TOK], BF16) for i in range(2)] for c in range(2)]
    vT = [sb(f"vT{i}", [128, 32, 128], BF16) for i in range(2)]
    gT = [sb(f"gT{i}", [128, TOK], BF16) for i in range(2)]
    for c in range(2):
        for i in range(2):
            K.op(dve, lambda: V.memset(kTm[c][i][64:128, :], 0.0), [], [kTm[c][i].buf])
            K.op(dve, lambda: V.memset(qTm[c][i][64:128, :], 0.0), [], [qTm[c][i].buf])
            K.dma(pq, kTm[c][i][64:68, :], c_kpos[:, :], [], [kTm[c][i].buf])
    NSB = 6
    LAG = 4
    sbt = [sb(f"sbt{i}", [128, 512]) for i in range(3)]
    ptt = [sb(f"ptt{i}", [128, 512], BF16) for i in range(NSB)]
    ep_ = {n: sb("ep_" + n, [128, 512]) for n in ["r0", "t0", "r1", "t1", "d", "dsq", "rs", "y1"]}
    SC = [PS[0], PS[1], PS[2]]
    OT = [PS[3], PS[4]]
    LT = [PS[5], PS[6]]
    MSB = PS[7]

    ev = {n: sb("ev_" + n, [128, 512]) for n in ["o0", "o1", "l0", "l1"]}

    def att_load(h):
        i = h % 2
        for c in range(2):
            K.dma(sp, kTm[c][i][0:64, :], kc[h][64 * c:64 * c + 64, :], [kc[h].buf], [kTm[c][i].buf])
            K.dma(sp, qTm[c][i][0:64, :], qb[h][64 * c:64 * c + 64, :], [qb[h].buf], [qTm[c][i].buf])
            K.dma(pq, qTm[c][i][64:68, :], c_qpos[4 * h:4 * h + 4, :], [], [qTm[c][i].buf])
        K.dma(sp, vT[i][:, :, :], vc[:, h * 128:(h + 1) * 128].rearrange("(t p) v -> p t v", p=128),
              [vc.buf], [vT[i].buf])
        K.dma(sp, gT[i][:, :], gb[h][:, :], [gb[h].buf], [gT[i].buf])

    def att_head(h):
        i = h % 2
        slope = 2.0 ** (-(h + 1))
        for qi in range(4):
            nfull = 16 + 4 * qi
            nkb = nfull + 4
            items = [(c, kb) for c in range(2) for kb in range(nkb)]
            n = len(items)

            def col0(kb):
                return 128 * (kb - nfull) if kb >= nfull else 0

            def qk(j):
                c, kb = items[j]
                bank = SC[j % 3]
                c0 = col0(kb)
                K.mm(bank.ap[:, c0:512], kTm[c][i][:, kb * 128:(kb + 1) * 128], qTm[c][i][:, qi * 512 + c0:(qi + 1) * 512], True, True,
                     [kTm[c][i].buf, qTm[c][i].buf], [bank.buf], inc=True)

            def soft(j):
                c, kb = items[j]
                bank = SC[j % 3]
                p_ = ptt[j % NSB]
                if kb < nfull:
                    if kb < 16:
                        K.actf(p_[:, :], bank.ap[:, :], AF.Exp, [bank.buf, cmask.buf], [p_.buf], bias=cmask[:, 0:1])
                    else:
                        K.actf(p_[:, :], bank.ap[:, :], AF.Exp, [bank.buf], [p_.buf])
                else:
                    s_ = sbt[j % 3]
                    bi = kb - nfull
                    c0 = col0(kb)
                    K.op(dve, lambda: V.scalar_tensor_tensor(s_[:, c0:512], dgT[:, bi * 512 + c0:(bi + 1) * 512], slope, bank.ap[:, c0:512],
                                                             ALU.mult, ALU.add), [dgT.buf, bank.buf], [s_.buf])
                    K.actf(p_[:, c0:512], s_[:, c0:512], AF.Exp, [s_.buf], [p_.buf])

            def pvm(j):
                c, kb = items[j]
                p_ = ptt[j % NSB]
                c0 = col0(kb)
                K.mm(OT[c].ap[:, c0:512], vT[i][:, kb, :], p_[:, c0:512], kb == 0, kb == nkb - 1,
                     [vT[i].buf, p_.buf], [OT[c].buf], inc=False)
                K.mm(LT[c].ap[:, c0:512], ONESB, p_[:, c0:512], kb == 0, kb == nkb - 1,
                     [cb.buf, p_.buf], [LT[c].buf], inc=True)

            for j in range(n + LAG):
                if j < n:
                    qk(j)
                if 1 <= j <= n:
                    soft(j - 1)
                if j >= LAG:
                    pvm(j - LAG)
                if pend[0] is not None and j in (12, 14):
                    if next(pend[0], "done") == "done":
                        pend[0] = None
            if pend[0] is not None:
                for _ in pend[0]:
                    pass
                pend[0] = None
            K.op(dve, lambda: V.tensor_copy(ev["l0"][:, :], LT[0].ap[:, :]), [LT[0].buf], [ev["l0"].buf])
            K.op(dve, lambda: V.tensor_copy(ev["o0"][:, :], OT[0].ap[:, :]), [OT[0].buf], [ev["o0"].buf])
            K.op(dve, lambda: V.tensor_copy(ev["l1"][:, :], LT[1].ap[:, :]), [LT[1].buf], [ev["l1"].buf])
            K.op(dve, lambda: V.tensor_copy(ev["o1"][:, :], OT[1].ap[:, :]), [OT[1].buf], [ev["o1"].buf])
            pend[0] = epilogue(h, i, qi)
            next(pend[0])
            if qi == 3:
                for _ in pend[0]:
                    pass
                pend[0] = None

    pend = [None]

    def epilogue(h, i, qi):
        e = ep_
        K.op(dve, lambda: V.reciprocal(e["r0"][:, :], ev["l0"][:, :]), [ev["l0"].buf], [e["r0"].buf])
        K.op(dve, lambda: V.tensor_tensor(e["t0"][:, :], ev["o0"][:, :], e["r0"][:, :], ALU.mult),
             [ev["o0"].buf, e["r0"].buf], [e["t0"].buf])
        K.op(dve, lambda: V.reciprocal(e["r1"][:, :], ev["l1"][:, :]), [ev["l1"].buf], [e["r1"].buf])
        K.op(dve, lambda: V.tensor_tensor(e["t1"][:, :], ev["o1"][:, :], e["r1"][:, :], ALU.mult),
             [ev["o1"].buf, e["r1"].buf], [e["t1"].buf])
        K.op(dve, lambda: V.scalar_tensor_tensor(e["d"][:, :], e["t1"][:, :], NEGLAM, e["t0"][:, :], ALU.mult, ALU.add),
             [e["t1"].buf, e["t0"].buf, pv.buf], [e["d"].buf])
        yield
        K.actf(e["dsq"][:, :], e["d"][:, :], AF.Square, [e["d"].buf], [e["dsq"].buf])
        K.mm(MSB.ap[:, :], O128, e["dsq"][:, :], True, True, [cf.buf, e["dsq"].buf], [MSB.buf], inc=True)
        yield
        K.actf(e["dsq"][:, :], MSB.ap[:, :], AF.Ln, [MSB.buf], [e["dsq"].buf], bias=EPS)
        K.actf(e["rs"][:, :], e["dsq"][:, :], AF.Exp, [e["dsq"].buf], [e["rs"].buf], scale=-0.5)
        K.op(dve, lambda: V.scalar_tensor_tensor(e["y1"][:, :], e["d"][:, :], SUBLN, e["rs"][:, :], ALU.mult, ALU.mult),
             [e["d"].buf, e["rs"].buf, pv.buf], [e["y1"].buf])
        K.op(dve, lambda: V.tensor_tensor(ybT[h][:, qi * 512:(qi + 1) * 512], e["y1"][:, :],
                                          gT[i][:, qi * 512:(qi + 1) * 512], ALU.mult),
             [e["y1"].buf, gT[i].buf], [ybT[h].buf])

    att_load(0)
    for h in range(8):
        if h + 1 < 8:
            att_load(h + 1)
        att_head(h)

    if debug:
        dbg["ybT0"] = ybT[0]

    K.barrier()
    while len(K.stack) > n_att:
        K.stack.pop().__exit__(None, None, None)

    mixT = sb("mixT", [128, 16, TOK], BF16)
    yaT = [sb(f"yaT{h}", [128, TOK], BF16) for h in range(8)]
    for h in range(8):
        K.dma(sp, yaT[h][:, :], yas[h][:, :], [yas[h].buf], [yaT[h].buf])
    mbuf = [Buf(f"mx{n}") for n in range(4)]
    wab = [sb(f"wab{i}", [128, 2, 8, 128], BF16) for i in range(2)]
    gab = [sb(f"gab{i}", [128, 2, 512], BF16) for i in range(4)]
    m12 = [sb(f"m12{i}", [128, 2, 512]) for i in range(2)]
    wob = [sb(f"wob{i}", [128, 16, 512], BF16) for i in range(2)]
    xres = [sb(f"xres{i}", [128, 512]) for i in range(4)]
    osb = [sb(f"osb{i}", [128, 512]) for i in range(4)]
    def load_wo(cg):
        wg_ = wob[cg % 2]
        K.dma(pq, wg_[:, :, :], wo[:, cg * 512:(cg + 1) * 512].rearrange("(kc p) n -> p kc n", p=128), [], [wg_.buf])

    cnt = 0
    for j in range(16):
        if j in (4, 8):
            load_wo(j // 4 - 1)
        wj = wab[j % 2]
        K.dma(pq, wj[:, 0, :, :], wa[:, j * 128:(j + 1) * 128].rearrange("(kc p) n -> p kc n", p=128), [], [wj.buf])
        K.dma(pq, wj[:, 1, :, :], wb[:, j * 128:(j + 1) * 128].rearrange("(kc p) n -> p kc n", p=128), [], [wj.buf])
        for n in range(4):
            gj = gab[cnt % 4]
            mj = m12[cnt % 2]
            K.dma(K.aq, gj[:, 0, :], gs[j][:, n * 512:(n + 1) * 512], [gs[j].buf], [gj.buf])
            K.dma(K.aq, gj[:, 1, :], gs[16 + j][:, n * 512:(n + 1) * 512], [gs[16 + j].buf], [gj.buf])
            pa, pb_ = PS[(2 * cnt) % 4], PS[(2 * cnt + 1) % 4]
            for k in range(8):
                K.mm(pa.ap[:, :], wj[:, 0, k, :], yaT[k][:, n * 512:(n + 1) * 512], k == 0, k == 7,
                     [wj.buf, yaT[k].buf], [pa.buf], inc=(k == 7))
            for k in range(8):
                K.mm(pb_.ap[:, :], wj[:, 1, k, :], ybT[k][:, n * 512:(n + 1) * 512], k == 0, k == 7,
                     [wj.buf, ybT[k].buf], [pb_.buf], inc=(k == 7))
            K.op(dve, lambda: V.tensor_tensor(mj[:, 0, :], pa.ap[:, :], gj[:, 0, :], ALU.mult), [pa.buf, gj.buf], [mj.buf])
            K.op(dve, lambda: V.tensor_tensor(mj[:, 1, :], pb_.ap[:, :], gj[:, 1, :], ALU.mult), [pb_.buf, gj.buf], [mj.buf])
            K.op(dve, lambda: V.tensor_tensor(mixT[:, j, n * 512:(n + 1) * 512], mj[:, 0, :], mj[:, 1, :], ALU.add),
                 [mj.buf], [mbuf[n]])
            cnt += 1
    out_toks = []
    cnt = 0
    for cg in range(4):
        wg = wob[cg % 2]
        if cg >= 2:
            load_wo(cg)
        for t in range(16):
            xr = xres[cnt % 4]
            ob_ = osb[cnt % 4]
            bank = PS[4 + cnt % 4]
            K.dma(K.aq, xr[:, :], xo[t * 128:(t + 1) * 128, cg * 512:(cg + 1) * 512], [], [xr.buf])
            for k in range(16):
                K.mm(bank.ap[:, :], mixT[:, k, t * 128:(t + 1) * 128], wg[:, k, :], k == 0, k == 15,
                     [mbuf[t // 4], wg.buf], [bank.buf], inc=(k == 15))
            K.op(dve, lambda: V.tensor_tensor(ob_[:, :], bank.ap[:, :], xr[:, :], ALU.add), [bank.buf, xr.buf], [ob_.buf])
            out_toks.append(K.dma(sp, y[t * 128:(t + 1) * 128, cg * 512:(cg + 1) * 512], ob_[:, :], [ob_.buf], []))
            cnt += 1

    dbg_out = {}
    if debug:
        for name, t_ in dbg.items():
            src_ = t_[:, :]
            o = nc.dram_tensor("dbg_" + name, list(src_.shape), src_.dtype, kind="ExternalOutput").ap()
            out_toks.append(K.dma(sp, o, src_, [t_.buf], []))
            dbg_out[name] = "dbg_" + name
    for t_ in K.last_tokens():
        if t_.eng.dma_sems is not None:
            sp.wait(t_)
    K.close()
    return nc, dbg_out


def _prep(inputs):
    f = lambda a: np.ascontiguousarray(np.asarray(a, dtype=np.float32))
    x = f(inputs["x"])
    w_in = f(inputs["w_in"])[0]
    perm = []
    for h in range(8):
        for blk in range(4):
            perm.extend(range(blk * 1024 + h * 128, blk * 1024 + (h + 1) * 128))
    perm.extend(range(4096, NIN))
    w_perm = np.ascontiguousarray(w_in[:, np.array(perm)])
    cf, cb, kpos, qpos, dg = _consts()
    shared = {
        "w_in": w_perm,
        "wa": f(inputs["w_branch_a"])[0], "wb": f(inputs["w_branch_b"])[0], "wo": f(inputs["w_out"])[0],
        "norm_w": f(inputs["norm_w"])[0], "alb": f(inputs["a_lower_bound"]), "aon": f(inputs["a_out_norm"])[0],
        "bqn": f(inputs["b_q_norm"])[0], "bkn": f(inputs["b_k_norm"])[0],
        "blam": f(inputs["b_lambda"])[0].reshape(256), "bsub": f(inputs["b_subln"])[0],
        "c_f": cf, "c_b": cb, "c_kpos": kpos, "c_qpos": qpos.reshape(32, TOK), "c_dg": dg,
    }
    zeros = np.zeros((TOK, D), np.float32)
    in_maps = []
    for c in range(NCORES):
        b, s = c // 2, c % 2
        m = dict(shared)
        m["xo"] = np.ascontiguousarray(x[b, s * TOK:(s + 1) * TOK])
        m["xp"] = np.ascontiguousarray(x[b, 0:TOK]) if s == 1 else zeros
        m["c_mask"] = np.full((128, 1), 0.0 if s == 1 else -1.0e9, np.float32)
        in_maps.append(m)
    return in_maps


def kernel(**inputs):
    in_maps = _prep(inputs)
    nc, _ = build(debug=False)
    res = run_bass_kernel_spmd(nc, in_maps, core_ids=list(range(NCORES)))
    out = np.empty((4, 2 * TOK, D), np.float32)
    for c in range(NCORES):
        b, s = c // 2, c % 2
        out[b, s * TOK:(s + 1) * TOK] = res.results[c]["y"]
    return out
```

```python
import numpy as np
import concourse.bass as bass
import concourse.mybir as mybir
from concourse.bass_utils import run_bass_kernel_spmd

F32 = mybir.dt.float32
BF16 = mybir.dt.bfloat16
AF = mybir.ActivationFunctionType
ALU = mybir.AluOpType
AX = mybir.AxisListType

D = 2048
TOK = 2048
NIN = 12288
EPS = 1e-6
NCORES = 8
LAMBDA_INIT = 0.2
NDMA_SEMS = 8

C_A = 0
C_BQ = 4096
C_BK = 5120
C_BV = 6144
C_BG = 7168
C_GA = 8192
C_GB = 10240


class Tok:
    __slots__ = ("sem", "val", "eng")

    def __init__(self, sem, val, eng):
        self.sem, self.val, self.eng = sem, val, eng


class Buf:
    __slots__ = ("name", "w", "r")

    def __init__(self, name=""):
        self.name, self.w, self.r = name, None, {}


class T:
    def __init__(self, ap, name=""):
        self.ap = ap
        self.buf = Buf(name)

    def __getitem__(self, k):
        return self.ap[k]


class Eng:
    def __init__(self, name, h, sem=None, dma_sems=None, seen=None):
        self.name, self.h, self.sem = name, h, sem
        self.count = 0
        self.dma_sems = dma_sems
        self.n = 0
        self.seen = {} if seen is None else seen
        self.pending = []

    def wait(self, tok):
        assert tok.val is not None, "waiting on an unresolved PE token"
        if self.seen.get(tok.sem, 0) >= tok.val:
            return
        self.h.wait_ge(tok.sem, tok.val)
        self.seen[tok.sem] = tok.val


class Kern:
    def __init__(self, nc):
        self.nc = nc
        self.stack = []
        mk = lambda n: self.enter(nc.semaphore(n))
        self.pe = Eng("pe", nc.tensor, mk("s_pe"))
        self.act = Eng("act", nc.scalar, mk("s_act"))
        self.dve = Eng("dve", nc.vector, mk("s_dve"))
        self.sp = Eng("sp", nc.sync, dma_sems=[mk(f"s_sp{i}") for i in range(NDMA_SEMS)])
        self.pq = Eng("pq", nc.gpsimd, dma_sems=[mk(f"s_pq{i}") for i in range(NDMA_SEMS)])
        self.aq = Eng("aq", nc.scalar, dma_sems=[mk(f"s_aq{i}") for i in range(NDMA_SEMS)], seen=self.act.seen)
        self.engines = [self.pe, self.act, self.dve, self.sp, self.pq, self.aq]

    def enter(self, cm):
        v = cm.__enter__()
        self.stack.append(cm)
        return v

    def close(self):
        while self.stack:
            self.stack.pop().__exit__(None, None, None)

    def op(self, eng, fn, reads=(), writes=(), inc=True):
        is_dma = eng.dma_sems is not None
        for b in reads:
            w = b.w
            if w is not None and not (w.eng is eng and eng.name == "pe"):
                eng.wait(w)
        for b in writes:
            w = b.w
            if w is not None and (is_dma or w.eng is not eng):
                eng.wait(w)
            for t in b.r.values():
                if is_dma or t.eng is not eng:
                    eng.wait(t)
        if is_dma:
            slot = eng.n % NDMA_SEMS
            rnd = eng.n // NDMA_SEMS
            sem = eng.dma_sems[slot]
            if rnd > 0 and eng.seen.get(sem, 0) < 16 * rnd:
                eng.h.wait_ge(sem, 16 * rnd)
                eng.seen[sem] = 16 * rnd
            ins = fn()
            ins.then_inc(sem, 16)
            tok = Tok(sem, 16 * (rnd + 1), eng)
            eng.n += 1
        else:
            ins = fn()
            if inc:
                eng.count += 1
                ins.then_inc(eng.sem, 1)
                tok = Tok(eng.sem, eng.count, eng)
                for p in eng.pending:
                    p.val = eng.count
                eng.pending = []
            else:
                tok = Tok(eng.sem, None, eng)
                eng.pending.append(tok)
        for b in reads:
            b.r[(tok.sem, id(eng))] = tok
        for b in writes:
            b.w = tok
            b.r = {}
        return tok

    def last_tokens(self):
        toks = []
        for e in self.engines:
            if e.dma_sems is None:
                assert not e.pending
                if e.count:
                    toks.append(Tok(e.sem, e.count, e))
            else:
                for i, sem in enumerate(e.dma_sems):
                    k = (e.n - i + NDMA_SEMS - 1) // NDMA_SEMS
                    if k > 0:
                        toks.append(Tok(sem, 16 * k, e))
        return toks

    def barrier(self):
        toks = self.last_tokens()
        for e in self.engines:
            for t in toks:
                if t.eng is e and e.dma_sems is None:
                    continue
                e.wait(t)

    def mm(self, out, lhsT, rhs, start, stop, reads, writes, inc):
        return self.op(self.pe, lambda: self.nc.tensor.matmul(out, lhsT, rhs, start=start, stop=stop),
                       reads, writes, inc)

    def tr(self, out, in_, ident, reads, writes, inc=True):
        return self.op(self.pe, lambda: self.nc.tensor.transpose(out, in_, ident), reads, writes, inc)

    def actf(self, out, in_, func, reads, writes, bias=0.0, scale=1.0, accum_out=None):
        def f():
            kw = {}
            if accum_out is not None:
                kw["accum_out"] = accum_out
            return self.nc.scalar.activation(out, in_, func, bias=bias, scale=scale, **kw)
        return self.op(self.act, f, reads, writes)

    def dma(self, eng, out, in_, reads, writes):
        return self.op(eng, lambda: eng.h.dma_start(out=out, in_=in_), reads, writes)


def _consts():
    i = np.arange(128)[:, None]
    j = np.arange(128)[None, :]
    trip = (i <= j).astype(np.float32) - (i <= 63).astype(np.float32)
    triu = (i <= j).astype(np.float32)
    bd = ((i // 64) == (j // 64)).astype(np.float32) / 64.0
    o128 = np.full((128, 128), 1.0 / 128.0, np.float32)
    sel = np.concatenate([(i <= 63), (i >= 64)], axis=1).astype(np.float32)
    cf = np.concatenate([trip, triu, bd, o128, sel], axis=1).astype(np.float32)
    cb = np.concatenate([np.eye(128, dtype=np.float32), np.ones((128, 128), np.float32), trip, sel], axis=1)
    tk = np.arange(4096)
    kpos = np.stack([64.0 * (tk // 64), (tk % 64).astype(np.float64), np.ones(4096), np.ones(4096), np.zeros(4096)]).astype(np.float32)
    tq = TOK + np.arange(TOK)
    qpos = np.zeros((8, 5, TOK), np.float32)
    for h in range(8):
        sl = 2.0 ** (-(h + 1))
        qpos[h] = np.stack([np.full(TOK, sl), np.full(TOK, sl), -sl * 64.0 * (tq // 64), -sl * (tq % 64), np.ones(TOK)])
    jq = np.arange(512)[None, :]
    dg = []
    for bi in range(4):
        kp = 128 * bi + i
        allowed = (kp // 64) <= (jq // 64)
        fix = np.where(kp > jq, -2.0 * (kp - jq), 0.0)
        dg.append(np.where(allowed, fix, -1.0e9).astype(np.float32))
    dg = np.concatenate(dg, axis=1)
    return cf, cb, kpos, qpos, dg


def build(debug=False):
    nc = bass.Bass("TRN2", target_bir_lowering=False)
    K = Kern(nc)
    pe, act, dve, sp, pq = K.pe, K.act, K.dve, K.sp, K.pq
    V = nc.vector
    dt_in = lambda n, s: nc.dram_tensor(n, s, F32, kind="ExternalInput").ap()
    xo = dt_in("xo", [TOK, D])
    xp = dt_in("xp", [TOK, D])
    w_in = dt_in("w_in", [D, NIN])
    wa = dt_in("wa", [1024, D])
    wb = dt_in("wb", [1024, D])
    wo = dt_in("wo", [D, D])
    norm_w = dt_in("norm_w", [D])
    alb = dt_in("alb", [2, 1024])
    aon = dt_in("aon", [128])
    bqn = dt_in("bqn", [64])
    bkn = dt_in("bkn", [64])
    blam = dt_in("blam", [256])
    bsub = dt_in("bsub", [128])
    c_f = dt_in("c_f", [128, 514])
    c_b = dt_in("c_b", [128, 386])
    c_kpos = dt_in("c_kpos", [5, 4096])
    c_qpos = dt_in("c_qpos", [40, TOK])
    c_dg = dt_in("c_dg", [128, 2048])
    c_mask = dt_in("c_mask", [128, 1])
    y = nc.dram_tensor("y", [TOK, D], F32, kind="ExternalOutput").ap()
    scr = lambda n, s: T(nc.dram_tensor(n, s, BF16, kind="Internal").ap(), n)
    kc = [scr(f"kc{h}", [128, 2 * TOK]) for h in range(8)]
    vc = scr("vc", [2 * TOK, 1024])
    qb = [scr(f"qb{h}", [128, TOK]) for h in range(8)]
    gb = [scr(f"gb{h}", [128, TOK]) for h in range(8)]
    gs = [scr(f"gs{j}", [128, TOK]) for j in range(32)]
    yas = [scr(f"yas{h}", [128, TOK]) for h in range(8)]
    dbg = {}

    def sb(name, shape, dt=F32):
        return T(K.enter(nc.sbuf_tensor(name, shape, dt)), name)

    PS = [T(K.enter(nc.psum_tensor(f"ps{i}", [128, 512], F32)), f"ps{i}") for i in range(8)]

    cf = sb("cf", [128, 514])
    cb = sb("cb", [128, 386], BF16)
    cmask = sb("cmask", [128, 1])
    K.dma(sp, cf[:, :], c_f[:, :], [], [cf.buf])
    K.dma(pq, cb[:, :], c_b[:, :], [], [cb.buf])
    K.dma(sp, cmask[:, :], c_mask[:, :], [], [cmask.buf])
    TRIP, TRIU, BD, O128, SEL = (cf[:, 0:128], cf[:, 128:256], cf[:, 256:384], cf[:, 384:512], cf[:, 512:514])
    IDENT, ONESB, TRIPB, SELB = cb[:, 0:128], cb[:, 128:256], cb[:, 256:384], cb[:, 384:386]

    gainA = sb("gainA", [128, 128])
    K.dma(sp, gainA[:, :], aon.partition_broadcast(128), [], [gainA.buf])
    pv = sb("pv", [128, 16])
    pvr = sb("pvr", [128, 4])
    with nc.allow_non_contiguous_dma(reason="tiny parameter vectors"):
        for half in range(2):
            K.dma(sp, pvr[64 * half:64 * half + 64, 0:1], bqn.rearrange("(p o) -> p o", o=1), [], [pvr.buf])
            K.dma(sp, pvr[64 * half:64 * half + 64, 1:2], bkn.rearrange("(p o) -> p o", o=1), [], [pvr.buf])
        K.dma(sp, pvr[:, 2:3], bsub.rearrange("(p o) -> p o", o=1), [], [pvr.buf])
    lam4 = sb("lam4", [128, 256])
    K.dma(sp, lam4[:, :], blam.partition_broadcast(128), [], [lam4.buf])
    lamt = sb("lamt", [128, 128])
    K.op(dve, lambda: V.tensor_scalar(pv[:, 0:1], pvr[:, 0:1], 0.125, None, ALU.mult), [pvr.buf], [pv.buf])
    K.op(dve, lambda: V.tensor_copy(pv[:, 1:2], pvr[:, 1:2]), [pvr.buf], [pv.buf])
    K.op(dve, lambda: V.tensor_scalar(pv[:, 2:3], pvr[:, 2:3], 1.0 - LAMBDA_INIT, None, ALU.mult), [pvr.buf], [pv.buf])
    K.op(dve, lambda: V.tensor_tensor(lamt[:, 0:64], lam4[:, 0:64], lam4[:, 64:128], ALU.mult), [lam4.buf], [lamt.buf])
    K.op(dve, lambda: V.tensor_tensor(lamt[:, 64:128], lam4[:, 128:192], lam4[:, 192:256], ALU.mult), [lam4.buf], [lamt.buf])
    K.op(dve, lambda: V.reduce_sum(pv[:, 4:5], lamt[:, 0:64], axis=AX.X), [lamt.buf], [pv.buf])
    K.op(dve, lambda: V.reduce_sum(pv[:, 5:6], lamt[:, 64:128], axis=AX.X), [lamt.buf], [pv.buf])
    K.actf(pv[:, 6:8], pv[:, 4:6], AF.Exp, [pv.buf], [pv.buf])
    K.op(dve, lambda: V.tensor_tensor(pv[:, 8:9], pv[:, 7:8], pv[:, 6:7], ALU.subtract), [pv.buf], [pv.buf])
    K.op(dve, lambda: V.tensor_scalar(pv[:, 3:4], pv[:, 8:9], -LAMBDA_INIT, None, ALU.add), [pv.buf], [pv.buf])
    GQ, GK, SUBLN, NEGLAM = pv[:, 0:1], pv[:, 1:2], pv[:, 2:3], pv[:, 3:4]


    n_proj = len(K.stack)
    hT = sb("hT", [128, 16, TOK], BF16)
    hbuf = [[Buf(f"h{t}_{hf}") for hf in range(2)] for t in range(16)]
    nwB = sb("nwB", [128, D])
    K.dma(sp, nwB[:, :], norm_w.partition_broadcast(128), [], [nwB.buf])
    omlB = sb("omlB", [128, 1024])

    Wt = [sb(f"Wt{i}", [128, 16, 512], BF16) for i in range(2)]
    wcnt = [0]
    xt = [sb(f"xt{i}", [128, D]) for i in range(3)]
    xn = sb("xn", [128, D], BF16)
    junk = xn
    a01 = xt[0]
    K.dma(sp, a01[:, 0:1024], alb[0, :].partition_broadcast(128), [], [a01.buf])
    K.dma(sp, a01[:, 1024:2048], alb[1, :].partition_broadcast(128), [], [a01.buf])
    K.op(dve, lambda: V.tensor_tensor(a01[:, 0:1024], a01[:, 1024:2048], a01[:, 0:1024], ALU.subtract), [a01.buf], [a01.buf])
    K.actf(omlB[:, :], a01[:, 0:1024], AF.Sigmoid, [a01.buf], [omlB.buf])
    K.op(dve, lambda: V.tensor_scalar(omlB[:, :], omlB[:, :], -0.5, None, ALU.mult), [omlB.buf], [omlB.buf])
    st1 = sb("st1", [128, 8])
    Sst = [sb(f"S{h}", [128, 128]) for h in range(8)]
    stg = [sb(f"stg{i}", [128, 512], BF16) for i in range(4)]
    stgc = [0]
    nsq = [sb(f"nsq{i}", [128, 512]) for i in range(2)]
    nrs = [sb(f"nrs{i}", [128, 512]) for i in range(2)]
    ncnt = [0]
    raw = [sb(f"raw{i}", [128, 4, 512]) for i in range(2)]
    vbq = [sb(f"vbq{i}", [128, 4, 128], BF16) for i in range(2)]
    hq = {n: sb("hq_" + n, shp, dt) for n, shp, dt in [
        ("lf", [128, 4, 128], F32), ("lfh", [128, 4, 128], BF16), ("lfl", [128, 4, 128], BF16), ("ep", [128, 4, 128], F32), ("en", [128, 4, 128], F32),
        ("qt", [128, 4, 128], BF16), ("kt", [128, 4, 128], BF16), ("qkT", [128, 1024], BF16),
        ("pm", [128, 4, 128], BF16), ("ug", [128, 4, 128], F32), ("sa", [128, 5, 128], F32),
        ("sp", [128, 4, 128], BF16), ("osq", [128, 4, 128], F32), ("ya", [128, 4, 128], BF16),
        ("ebc", [128, 4, 2], F32), ("gc", [128, 4], F32), ("ss", [128, 4], F32), ("lnv", [128, 4], F32),
        ("rstd", [128, 4], F32), ("yat", [128, 512], BF16), ("gg", [128, 4, 128], F32)]}

    def build_front(t):
        if True:
            x_t = xt[t % 3]
            K.actf(junk[:, :], x_t[:, :], AF.Square, [x_t.buf], [xn.buf, st1.buf], accum_out=st1[:, 0:1])
            K.actf(st1[:, 1:2], st1[:, 0:1], AF.Sqrt, [st1.buf], [st1.buf], bias=EPS, scale=1.0 / D)
            K.op(dve, lambda: V.reciprocal(st1[:, 2:3], st1[:, 1:2]), [st1.buf], [st1.buf])
            K.op(dve, lambda: V.scalar_tensor_tensor(xn[:, :], x_t[:, :], st1[:, 2:3], nwB[:, :], ALU.mult, ALU.mult),
                 [x_t.buf, st1.buf, nwB.buf], [xn.buf])

    def build_back(t):
        if True:
            for half in range(2):
                bank = PS[half]
                pb = bank.ap[:, :].bitcast(BF16)
                for c in range(8):
                    kcix = half * 8 + c
                    K.tr(pb[:, c * 128:(c + 1) * 128], xn[:, kcix * 128:(kcix + 1) * 128], IDENT,
                         [xn.buf, cb.buf], [bank.buf], inc=(c == 7))
                src = pb.rearrange("p (a b) -> p a b", b=128)
                dst = hT[:, half * 8:half * 8 + 8, t * 128:(t + 1) * 128]
                if half == 0:
                    K.op(dve, lambda: V.tensor_copy(dst, src), [bank.buf], [hbuf[t][0]])
                else:
                    K.op(act, lambda: nc.scalar.copy(dst, src), [bank.buf], [hbuf[t][1]])

    def load_w(c0, ncols):
        w = Wt[wcnt[0] % 2]
        wcnt[0] += 1
        src = w_in[:, c0:c0 + ncols].rearrange("(kc p) n -> p kc n", p=128)
        K.dma(pq, w[:, :, 0:ncols], src, [], [w.buf])
        return w

    def gemm_fm(w, col, n, bank):
        for k in range(16):
            K.mm(bank.ap[:, :], w[:, k, col:col + 128], hT[:, k, n * 512:(n + 1) * 512], k == 0, k == 15,
                 [w.buf] + sum(hbuf[4 * n:4 * n + 4], []), [bank.buf], inc=(k == 15))

    def gemm_tm(w, col, ncols, t, bank):
        for k in range(16):
            K.mm(bank.ap[:, 0:ncols], hT[:, k, t * 128:(t + 1) * 128], w[:, k, col:col + ncols], k == 0, k == 15,
                 [w.buf] + hbuf[t], [bank.buf], inc=(k == 15))

    def next_stg():
        s = stg[stgc[0] % 4]
        stgc[0] += 1
        return s

    def qknorm(bank, gain, dst_dram, dst_ap):
        i = ncnt[0] % 2
        ncnt[0] += 1
        sq, rs = nsq[i], nrs[i]
        K.actf(sq[:, :], bank.ap[:, :], AF.Square, [bank.buf], [sq.buf])
        msb = PS[4]
        K.mm(msb.ap[:, :], BD, sq[:, :], True, True, [cf.buf, sq.buf], [msb.buf], inc=True)
        K.actf(rs[:, :], msb.ap[:, :], AF.Sqrt, [msb.buf], [rs.buf], bias=EPS)
        K.op(dve, lambda: V.reciprocal(rs[:, :], rs[:, :]), [rs.buf], [rs.buf])
        s = next_stg()
        K.op(dve, lambda: V.scalar_tensor_tensor(s[:, :], bank.ap[:, :], gain, rs[:, :], ALU.mult, ALU.mult),
             [bank.buf, rs.buf, pv.buf], [s.buf])
        K.dma(sp, dst_ap, s[:, :], [s.buf], [dst_dram.buf])

    gcnt = [0]

    def next_bank():
        b = PS[2 + gcnt[0] % 2]
        gcnt[0] += 1
        return b

    def proj_k(tok_off):
        for g in range(2):
            w = load_w(C_BK + 512 * g, 512)
            for m in range(4):
                h = 4 * g + m
                for n in range(4):
                    bank = next_bank()
                    gemm_fm(w, 128 * m, n, bank)
                    qknorm(bank, GK, kc[h], kc[h][:, tok_off + n * 512: tok_off + (n + 1) * 512])

    def proj_q():
        for g in range(2):
            w = load_w(C_BQ + 512 * g, 512)
            for m in range(4):
                h = 4 * g + m
                for n in range(4):
                    bank = next_bank()
                    gemm_fm(w, 128 * m, n, bank)
                    qknorm(bank, GQ, qb[h], qb[h][:, n * 512:(n + 1) * 512])

    def proj_act_fm(c0, func, dsts):
        ngroups = len(dsts) // 4
        for g in range(ngroups):
            w = load_w(c0 + 512 * g, 512)
            for m in range(4):
                for n in range(4):
                    bank = next_bank()
                    gemm_fm(w, 128 * m, n, bank)
                    s = next_stg()
                    K.actf(s[:, :], bank.ap[:, :], func, [bank.buf], [s.buf])
                    d = dsts[4 * g + m]
                    K.dma(sp, d[:, n * 512:(n + 1) * 512], s[:, :], [s.buf], [d.buf])

    def build_and_v(xsrc, tok_off):
        ws = [load_w(C_BV + 512 * g, 512) for g in range(2)]
        def load_x(t):
            K.dma(sp, xt[t % 3][:, :], xsrc[t * 128:(t + 1) * 128, :], [], [xt[t % 3].buf])
        load_x(0)
        load_x(1)
        load_x(2)
        build_front(0)
        build_back(0)
        for t in range(16):
            if t + 1 < 16:
                build_front(t + 1)
                if t + 3 < 16:
                    load_x(t + 3)
            for g in range(2):
                w = ws[g]
                bank = next_bank()
                gemm_tm(w, 0, 512, t, bank)
                s = next_stg()
                K.op(dve, lambda: V.tensor_copy(s[:, :], bank.ap[:, :]), [bank.buf], [s.buf])
                K.dma(sp, vc[tok_off + t * 128: tok_off + (t + 1) * 128, g * 512:(g + 1) * 512], s[:, :],
                      [s.buf], [vc.buf])
            if t + 1 < 16:
                build_back(t + 1)

    GB = [PS[0], PS[1], PS[2], PS[3]]
    X1, X2, X3, X4 = PS[4], PS[5], PS[6], PS[7]

    def bc4(ap2d):
        return ap2d.unsqueeze(1).broadcast_to([128, 4, 128])

    def pass1(P, vb, own):
        zc, ic = (1, 2) if own else (0, 1)
        z = P[:, :, zc * 128:(zc + 1) * 128]
        K.actf(z, z, AF.Tanh, [P.buf], [P.buf], scale=0.5)
        if own:
            q = P[:, :, 0:128]
            gg = P[:, :, 384:512]
            K.actf(q, q, AF.Silu, [P.buf], [P.buf])
            K.actf(gg, gg, AF.Silu, [P.buf], [P.buf])
        K.actf(vb[:, :, :], P[:, :, ic * 128:(ic + 1) * 128], AF.Copy, [P.buf], [vb.buf])

    def pass2(h, qd, P, vb, own):
        g = hq
        S = Sst[h]
        zc = 1 if own else 0
        kk = P[:, :, zc * 128:(zc + 1) * 128]
        K.op(dve, lambda: V.scalar_tensor_tensor(kk, kk, 1.0, bc4(omlB[:, h * 128:(h + 1) * 128]), ALU.subtract, ALU.mult),
             [P.buf, omlB.buf], [P.buf])
        K.actf(g["lf"][:, :, :], kk, AF.Ln, [P.buf], [g["lf"].buf], bias=1.0, scale=-1.0)
        K.op(dve, lambda: V.tensor_copy(g["lfh"][:, :, :], g["lf"][:, :, :]), [g["lf"].buf], [g["lfh"].buf])
        K.op(dve, lambda: V.tensor_tensor(g["lfl"][:, :, :], g["lf"][:, :, :], g["lfh"][:, :, :], ALU.subtract),
             [g["lf"].buf, g["lfh"].buf], [g["lfl"].buf])
        yield
        for t in range(4):
            K.mm(X1.ap[:, t * 128:(t + 1) * 128], TRIPB, g["lfh"][:, t, :], True, False, [cb.buf, g["lfh"].buf], [X1.buf], inc=False)
            K.mm(X1.ap[:, t * 128:(t + 1) * 128], TRIPB, g["lfl"][:, t, :], False, True, [cb.buf, g["lfl"].buf], [X1.buf], inc=(t == 3))
        for t in range(4):
            K.mm(X2.ap[:, 2 * t:2 * t + 2], g["lfh"][:, t, :], SELB, True, False, [cb.buf, g["lfh"].buf], [X2.buf], inc=False)
            K.mm(X2.ap[:, 2 * t:2 * t + 2], g["lfl"][:, t, :], SELB, False, True, [cb.buf, g["lfl"].buf], [X2.buf], inc=(t == 3))
        yield
        x1v = X1.ap[:, :].rearrange("p (a b) -> p a b", b=128)
        K.actf(g["en"][:, :, :], x1v, AF.Exp, [X1.buf], [g["en"].buf], scale=-1.0)
        K.actf(g["ebc"][:, :, :], X2.ap[:, 0:8].rearrange("p (a b) -> p a b", b=2), AF.Exp, [X2.buf], [g["ebc"].buf])
        if own:
            K.actf(g["ep"][:, :, :], x1v, AF.Exp, [X1.buf], [g["ep"].buf])
        K.op(dve, lambda: V.tensor_tensor(g["kt"][:, :, :], kk, g["en"][:, :, :], ALU.mult), [P.buf, g["en"].buf], [g["kt"].buf])
        if own:
            K.op(dve, lambda: V.tensor_tensor(g["qt"][:, :, :], P[:, :, 0:128], g["ep"][:, :, :], ALU.mult),
                 [P.buf, g["ep"].buf], [g["qt"].buf])
            K.op(dve, lambda: V.tensor_tensor(g["gg"][:, :, :], P[:, :, 384:512], bc4(gainA[:, :]), ALU.mult),
                 [P.buf, gainA.buf], [g["gg"].buf])
        K.op(dve, lambda: V.tensor_tensor(g["gc"][:, 0:3], g["ebc"][:, 1:4, 0], g["ebc"][:, 0:3, 1], ALU.mult),
             [g["ebc"].buf], [g["gc"].buf])
        K.op(dve, lambda: V.tensor_copy(g["gc"][:, 3:4], g["ebc"][:, 3, 1:2]), [g["ebc"].buf], [g["gc"].buf])
        K.op(dve, lambda: V.tensor_scalar(g["sa"][:, 0, :], S[:, :], g["ebc"][:, 0, 0:1], None, ALU.mult),
             [S.buf, g["ebc"].buf], [g["sa"].buf])
        yield
        for t in range(4):
            K.mm(X3.ap[:, t * 128:(t + 1) * 128], g["kt"][:, t, :], vb[:, t, :], True, True, [g["kt"].buf, vb.buf], [X3.buf], inc=(t == 3))
        if own:
            x2b = X2.ap[:, :].bitcast(BF16)
            for t in range(4):
                K.tr(x2b[:, t * 128:(t + 1) * 128], g["qt"][:, t, :], IDENT, [g["qt"].buf, cb.buf], [X2.buf], inc=False)
            for t in range(4):
                K.tr(x2b[:, 512 + t * 128:512 + (t + 1) * 128], g["kt"][:, t, :], IDENT, [g["kt"].buf, cb.buf], [X2.buf], inc=(t == 3))
        yield
        K.op(dve, lambda: V.tensor_tensor(g["ug"][:, :, :], X3.ap[:, :].rearrange("p (a b) -> p a b", b=128),
                                          g["gc"][:, 0:4].unsqueeze(2).broadcast_to([128, 4, 128]), ALU.mult),
             [X3.buf, g["gc"].buf], [g["ug"].buf])
        if own:
            K.actf(g["qkT"][:, :], x2b[:, :], AF.Copy, [X2.buf], [g["qkT"].buf])
        yield
        for t in range(4):
            dst = g["sa"][:, t + 1, :] if t < 3 else S[:, :]
            dbuf = g["sa"].buf if t < 3 else S.buf
            K.op(dve, lambda: V.scalar_tensor_tensor(dst, g["sa"][:, t, :], g["gc"][:, t:t + 1], g["ug"][:, t, :], ALU.mult, ALU.add),
                 [g["sa"].buf, g["gc"].buf, g["ug"].buf], [dbuf])
        if not own:
            return
        K.actf(g["sp"][:, :, :], g["sa"][:, 0:4, :], AF.Copy, [g["sa"].buf], [g["sp"].buf])
        for t in range(4):
            K.mm(X1.ap[:, t * 128:(t + 1) * 128], g["qkT"][:, 512 + t * 128:512 + (t + 1) * 128], g["qkT"][:, t * 128:(t + 1) * 128],
                 True, True, [g["qkT"].buf], [X1.buf], inc=(t == 3))
        yield
        K.op(dve, lambda: V.tensor_tensor(g["pm"][:, :, :], x1v, bc4(TRIU), ALU.mult), [X1.buf, cf.buf], [g["pm"].buf])
        yield
        for t in range(4):
            K.mm(X4.ap[:, t * 128:(t + 1) * 128], g["pm"][:, t, :], vb[:, t, :], True, False, [g["pm"].buf, vb.buf], [X4.buf], inc=False)
            K.mm(X4.ap[:, t * 128:(t + 1) * 128], g["qkT"][:, t * 128:(t + 1) * 128], g["sp"][:, t, :], False, True,
                 [g["qkT"].buf, g["sp"].buf], [X4.buf], inc=(t == 3))
        yield
        x4v = X4.ap[:, :].rearrange("p (a b) -> p a b", b=128)
        K.actf(g["osq"][:, :, :], x4v, AF.Square, [X4.buf], [g["osq"].buf])
        K.op(dve, lambda: V.reduce_sum(g["ss"][:, :], g["osq"][:, :, :], axis=AX.X), [g["osq"].buf], [g["ss"].buf])
        K.actf(g["lnv"][:, :], g["ss"][:, :], AF.Ln, [g["ss"].buf], [g["lnv"].buf], bias=EPS, scale=1.0 / 128.0)
        K.actf(g["rstd"][:, :], g["lnv"][:, :], AF.Exp, [g["lnv"].buf], [g["rstd"].buf], scale=-0.5)
        for t in range(4):
            K.op(dve, lambda: V.scalar_tensor_tensor(g["ya"][:, t, :], X4.ap[:, t * 128:(t + 1) * 128], g["rstd"][:, t:t + 1],
                                                     g["gg"][:, t, :], ALU.mult, ALU.mult),
                 [X4.buf, g["rstd"].buf, g["gg"].buf], [g["ya"].buf])
        yield
        x3b = X3.ap[:, :].bitcast(BF16)
        for t in range(4):
            K.tr(x3b[:, t * 128:(t + 1) * 128], g["ya"][:, t, :], IDENT, [g["ya"].buf, cb.buf], [X3.buf], inc=(t == 3))
        yield
        K.actf(g["yat"][:, :], x3b[:, 0:512], AF.Copy, [X3.buf], [g["yat"].buf])
        K.dma(sp, yas[h][:, qd * 512:(qd + 1) * 512], g["yat"][:, :], [g["yat"].buf], [yas[h].buf])

    def proj_a(own):
        ncols = 512 if own else 256
        sched = [1, 2, 2, 2] if own else [1, 2, 2, 1]
        tsched = [2, 0, 0, 0]
        cur, tail = None, None
        qcount = 0

        def adv(gen, k):
            if gen is None:
                return None
            for _ in range(k):
                if next(gen, "done") == "done":
                    return None
            return gen

        for h in range(8):
            w = load_w(C_A + 512 * h + (0 if own else 128), ncols)
            for qd in range(4):
                P, vb = raw[qcount % 2], vbq[qcount % 2]
                qcount += 1
                for tt in range(4):
                    t = 4 * qd + tt
                    bank = GB[tt]
                    gemm_tm(w, 0, ncols, t, bank)
                    K.op(dve, lambda: V.tensor_copy(P[:, tt, 0:ncols], bank.ap[:, 0:ncols]), [bank.buf], [P.buf])
                    tail = adv(tail, tsched[tt])
                    if tt == 1 and tail is not None:
                        for _ in tail:
                            pass
                        tail = None
                    cur = adv(cur, sched[tt])
                if own:
                    cur = adv(cur, 2)
                tail = cur
                pass1(P, vb, own)
                cur = pass2(h, qd, P, vb, own)
        for gen in (tail, cur):
            if gen is not None:
                for _ in gen:
                    pass

    for h in range(8):
        K.op(dve, lambda: V.memset(Sst[h][:, :], 0.0), [], [Sst[h].buf])

    build_and_v(xp, 0)
    proj_k(0)
    proj_a(False)
    build_and_v(xo, TOK)
    proj_k(TOK)
    proj_q()
    proj_act_fm(C_BG, AF.Silu, gb)
    proj_act_fm(C_GA, AF.Sigmoid, gs)
    proj_a(True)


    K.barrier()
    while len(K.stack) > n_proj:
        K.stack.pop().__exit__(None, None, None)

    ybT = [sb(f"ybT{h}", [128, TOK], BF16) for h in range(8)]
    n_att = len(K.stack)
    dgT = sb("dgT", [128, 2048])
    K.dma(sp, dgT[:, :], c_dg[:, :], [], [dgT.buf])
    kTm = [[sb(f"kT{c}_{i}", [128, 2 * TOK], BF16) for i in range(2)] for c in range(2)]
    qTm = [[sb(f"qT{c}_{i}", [128, TOK], BF16) for i in range(2)] for c in range(2)]
    vT = [sb(f"vT{i}", [128, 32, 128], BF16) for i in range(2)]
    gT = [sb(f"gT{i}", [128, TOK], BF16) for i in range(2)]
    for c in range(2):
        for i in range(2):
            K.op(dve, lambda: V.memset(kTm[c][i][64:128, :], 0.0), [], [kTm[c][i].buf])
            K.op(dve, lambda: V.memset(qTm[c][i][64:128, :], 0.0), [], [qTm[c][i].buf])
            K.dma(pq, kTm[c][i][64:69, :], c_kpos[:, :], [], [kTm[c][i].buf])
    NSB = 6
    LAG = 4
    sbt = [sb(f"sbt{i}", [128, 512]) for i in range(3)]
    ptt = [sb(f"ptt{i}", [128, 512], BF16) for i in range(NSB)]
    ep_ = {n: sb("ep_" + n, [128, 512]) for n in ["r0", "t0", "r1", "t1", "d", "dsq", "rs", "y1"]}
    SC = [PS[0], PS[1], PS[2]]
    OT = [PS[3], PS[4]]
    LT = [PS[5], PS[6]]
    MSB = PS[7]

    ev = {n: sb("ev_" + n, [128, 512]) for n in ["o0", "o1", "l0", "l1"]}

    def att_load(h):
        i = h % 2
        for c in range(2):
            K.dma(sp, kTm[c][i][0:64, :], kc[h][64 * c:64 * c + 64, :], [kc[h].buf], [kTm[c][i].buf])
            K.dma(sp, qTm[c][i][0:64, :], qb[h][64 * c:64 * c + 64, :], [qb[h].buf], [qTm[c][i].buf])
            K.dma(pq, qTm[c][i][64:69, :], c_qpos[5 * h:5 * h + 5, :], [], [qTm[c][i].buf])
        K.dma(sp, vT[i][:, :, :], vc[:, h * 128:(h + 1) * 128].rearrange("(t p) v -> p t v", p=128),
              [vc.buf], [vT[i].buf])
        K.dma(sp, gT[i][:, :], gb[h][:, :], [gb[h].buf], [gT[i].buf])

    def att_head(h):
        i = h % 2
        slope = 2.0 ** (-(h + 1))
        for qi in range(4):
            nfull = 16 + 4 * qi
            nkb = nfull + 4
            items = [(c, kb) for c in range(2) for kb in range(nkb)]
            n = len(items)

            def col0(kb):
                return 128 * (kb - nfull) if kb >= nfull else 0

            def qk(j):
                c, kb = items[j]
                bank = SC[j % 3]
                c0 = col0(kb)
                K.mm(bank.ap[:, c0:512], kTm[c][i][:, kb * 128:(kb + 1) * 128], qTm[c][i][:, qi * 512 + c0:(qi + 1) * 512], True, True,
                     [kTm[c][i].buf, qTm[c][i].buf], [bank.buf], inc=True)

            def soft(j):
                c, kb = items[j]
                bank = SC[j % 3]
                p_ = ptt[j % NSB]
                if kb < nfull:
                    K.actf(p_[:, :], bank.ap[:, :], AF.Exp, [bank.buf], [p_.buf])
                else:
                    s_ = sbt[j % 3]
                    bi = kb - nfull
                    c0 = col0(kb)
                    K.op(dve, lambda: V.scalar_tensor_tensor(s_[:, c0:512], dgT[:, bi * 512 + c0:(bi + 1) * 512], slope, bank.ap[:, c0:512],
                                                             ALU.mult, ALU.add), [dgT.buf, bank.buf], [s_.buf])
                    K.actf(p_[:, c0:512], s_[:, c0:512], AF.Exp, [s_.buf], [p_.buf])

            def pvm(j):
                c, kb = items[j]
                p_ = ptt[j % NSB]
                c0 = col0(kb)
                K.mm(OT[c].ap[:, c0:512], vT[i][:, kb, :], p_[:, c0:512], kb == 0, kb == nkb - 1,
                     [vT[i].buf, p_.buf], [OT[c].buf], inc=False)
                K.mm(LT[c].ap[:, c0:512], ONESB, p_[:, c0:512], kb == 0, kb == nkb - 1,
                     [cb.buf, p_.buf], [LT[c].buf], inc=True)

            for j in range(n + LAG):
                if j < n:
                    qk(j)
                if 1 <= j <= n:
                    soft(j - 1)
                if j >= LAG:
                    pvm(j - LAG)
                if pend[0] is not None and j in (12, 14):
                    if next(pend[0], "done") == "done":
                        pend[0] = None
            if pend[0] is not None:
                for _ in pend[0]:
                    pass
                pend[0] = None
            K.op(dve, lambda: V.tensor_copy(ev["l0"][:, :], LT[0].ap[:, :]), [LT[0].buf], [ev["l0"].buf])
            K.op(dve, lambda: V.tensor_copy(ev["o0"][:, :], OT[0].ap[:, :]), [OT[0].buf], [ev["o0"].buf])
            K.op(dve, lambda: V.tensor_copy(ev["l1"][:, :], LT[1].ap[:, :]), [LT[1].buf], [ev["l1"].buf])
            K.op(dve, lambda: V.tensor_copy(ev["o1"][:, :], OT[1].ap[:, :]), [OT[1].buf], [ev["o1"].buf])
            pend[0] = epilogue(h, i, qi)
            next(pend[0])
            if qi == 3:
                for _ in pend[0]:
                    pass
                pend[0] = None

    pend = [None]

    def epilogue(h, i, qi):
        e = ep_
        K.op(dve, lambda: V.reciprocal(e["r0"][:, :], ev["l0"][:, :]), [ev["l0"].buf], [e["r0"].buf])
        K.op(dve, lambda: V.tensor_tensor(e["t0"][:, :], ev["o0"][:, :], e["r0"][:, :], ALU.mult),
             [ev["o0"].buf, e["r0"].buf], [e["t0"].buf])
        K.op(dve, lambda: V.reciprocal(e["r1"][:, :], ev["l1"][:, :]), [ev["l1"].buf], [e["r1"].buf])
        K.op(dve, lambda: V.tensor_tensor(e["t1"][:, :], ev["o1"][:, :], e["r1"][:, :], ALU.mult),
             [ev["o1"].buf, e["r1"].buf], [e["t1"].buf])
        K.op(dve, lambda: V.scalar_tensor_tensor(e["d"][:, :], e["t1"][:, :], NEGLAM, e["t0"][:, :], ALU.mult, ALU.add),
             [e["t1"].buf, e["t0"].buf, pv.buf], [e["d"].buf])
        yield
        K.actf(e["dsq"][:, :], e["d"][:, :], AF.Square, [e["d"].buf], [e["dsq"].buf])
        K.mm(MSB.ap[:, :], O128, e["dsq"][:, :], True, True, [cf.buf, e["dsq"].buf], [MSB.buf], inc=True)
        yield
        K.actf(e["dsq"][:, :], MSB.ap[:, :], AF.Ln, [MSB.buf], [e["dsq"].buf], bias=EPS)
        K.actf(e["rs"][:, :], e["dsq"][:, :], AF.Exp, [e["dsq"].buf], [e["rs"].buf], scale=-0.5)
        K.op(dve, lambda: V.scalar_tensor_tensor(e["y1"][:, :], e["d"][:, :], SUBLN, e["rs"][:, :], ALU.mult, ALU.mult),
             [e["d"].buf, e["rs"].buf, pv.buf], [e["y1"].buf])
        K.op(dve, lambda: V.tensor_tensor(ybT[h][:, qi * 512:(qi + 1) * 512], e["y1"][:, :],
                                          gT[i][:, qi * 512:(qi + 1) * 512], ALU.mult),
             [e["y1"].buf, gT[i].buf], [ybT[h].buf])

    att_load(0)
    for h in range(8):
        if h + 1 < 8:
            att_load(h + 1)
        att_head(h)

    if debug:
        dbg["ybT0"] = ybT[0]

    K.barrier()
    while len(K.stack) > n_att:
        K.stack.pop().__exit__(None, None, None)

    mixT = sb("mixT", [128, 16, TOK], BF16)
    yaT = [sb(f"yaT{h}", [128, TOK], BF16) for h in range(8)]
    for h in range(8):
        K.dma(sp, yaT[h][:, :], yas[h][:, :], [yas[h].buf], [yaT[h].buf])
    mbuf = [Buf(f"mx{n}") for n in range(4)]
    wab = [sb(f"wab{i}", [128, 2, 8, 128], BF16) for i in range(2)]
    gab = [sb(f"gab{i}", [128, 2, 512], BF16) for i in range(4)]
    m12 = [sb(f"m12{i}", [128, 2, 512]) for i in range(2)]
    wob = [sb(f"wob{i}", [128, 16, 512], BF16) for i in range(2)]
    xres = [sb(f"xres{i}", [128, 512]) for i in range(4)]
    osb = [sb(f"osb{i}", [128, 512]) for i in range(4)]
    def load_wo(cg):
        wg_ = wob[cg % 2]
        K.dma(pq, wg_[:, :, :], wo[:, cg * 512:(cg + 1) * 512].rearrange("(kc p) n -> p kc n", p=128), [], [wg_.buf])

    cnt = 0
    for j in range(16):
        if j in (4, 8):
            load_wo(j // 4 - 1)
        wj = wab[j % 2]
        K.dma(pq, wj[:, 0, :, :], wa[:, j * 128:(j + 1) * 128].rearrange("(kc p) n -> p kc n", p=128), [], [wj.buf])
        K.dma(pq, wj[:, 1, :, :], wb[:, j * 128:(j + 1) * 128].rearrange("(kc p) n -> p kc n", p=128), [], [wj.buf])
        for n in range(4):
            gj = gab[cnt % 4]
            mj = m12[cnt % 2]
            K.dma(K.aq, gj[:, 0, :], gs[j][:, n * 512:(n + 1) * 512], [gs[j].buf], [gj.buf])
            K.dma(K.aq, gj[:, 1, :], gs[16 + j][:, n * 512:(n + 1) * 512], [gs[16 + j].buf], [gj.buf])
            pa, pb_ = PS[(2 * cnt) % 4], PS[(2 * cnt + 1) % 4]
            for k in range(8):
                K.mm(pa.ap[:, :], wj[:, 0, k, :], yaT[k][:, n * 512:(n + 1) * 512], k == 0, k == 7,
                     [wj.buf, yaT[k].buf], [pa.buf], inc=(k == 7))
            for k in range(8):
                K.mm(pb_.ap[:, :], wj[:, 1, k, :], ybT[k][:, n * 512:(n + 1) * 512], k == 0, k == 7,
                     [wj.buf, ybT[k].buf], [pb_.buf], inc=(k == 7))
            K.op(dve, lambda: V.tensor_tensor(mj[:, 0, :], pa.ap[:, :], gj[:, 0, :], ALU.mult), [pa.buf, gj.buf], [mj.buf])
            K.op(dve, lambda: V.tensor_tensor(mj[:, 1, :], pb_.ap[:, :], gj[:, 1, :], ALU.mult), [pb_.buf, gj.buf], [mj.buf])
            K.op(dve, lambda: V.tensor_tensor(mixT[:, j, n * 512:(n + 1) * 512], mj[:, 0, :], mj[:, 1, :], ALU.add),
                 [mj.buf], [mbuf[n]])
            cnt += 1
    out_toks = []
    cnt = 0
    for cg in range(4):
        wg = wob[cg % 2]
        if cg >= 2:
            load_wo(cg)
        for t in range(16):
            xr = xres[cnt % 4]
            ob_ = osb[cnt % 4]
            bank = PS[4 + cnt % 4]
            K.dma(K.aq, xr[:, :], xo[t * 128:(t + 1) * 128, cg * 512:(cg + 1) * 512], [], [xr.buf])
            for k in range(16):
                K.mm(bank.ap[:, :], mixT[:, k, t * 128:(t + 1) * 128], wg[:, k, :], k == 0, k == 15,
                     [mbuf[t // 4], wg.buf], [bank.buf], inc=(k == 15))
            K.op(dve, lambda: V.tensor_tensor(ob_[:, :], bank.ap[:, :], xr[:, :], ALU.add), [bank.buf, xr.buf], [ob_.buf])
            out_toks.append(K.dma(sp, y[t * 128:(t + 1) * 128, cg * 512:(cg + 1) * 512], ob_[:, :], [ob_.buf], []))
            cnt += 1

    dbg_out = {}
    if debug:
        for name, t_ in dbg.items():
            src_ = t_[:, :]
            o = nc.dram_tensor("dbg_" + name, list(src_.shape), src_.dtype, kind="ExternalOutput").ap()
            out_toks.append(K.dma(sp, o, src_, [t_.buf], []))
            dbg_out[name] = "dbg_" + name
    for t_ in K.last_tokens():
        if t_.eng.dma_sems is not None:
            sp.wait(t_)
    K.close()
    return nc, dbg_out


def _prep(inputs):
    f = lambda a: np.ascontiguousarray(np.asarray(a, dtype=np.float32))
    x = f(inputs["x"])
    w_in = f(inputs["w_in"])[0]
    perm = []
    for h in range(8):
        for blk in range(4):
            perm.extend(range(blk * 1024 + h * 128, blk * 1024 + (h + 1) * 128))
    perm.extend(range(4096, NIN))
    w_perm = np.ascontiguousarray(w_in[:, np.array(perm)])
    cf, cb, kpos, qpos, dg = _consts()
    shared = {
        "w_in": w_perm,
        "wa": f(inputs["w_branch_a"])[0], "wb": f(inputs["w_branch_b"])[0], "wo": f(inputs["w_out"])[0],
        "norm_w": f(inputs["norm_w"])[0], "alb": f(inputs["a_lower_bound"]), "aon": f(inputs["a_out_norm"])[0],
        "bqn": f(inputs["b_q_norm"])[0], "bkn": f(inputs["b_k_norm"])[0],
        "blam": f(inputs["b_lambda"])[0].reshape(256), "bsub": f(inputs["b_subln"])[0],
        "c_f": cf, "c_b": cb, "c_qpos": qpos.reshape(40, TOK), "c_dg": dg,
    }
    zeros = np.zeros((TOK, D), np.float32)
    in_maps = []
    for c in range(NCORES):
        b, s = c // 2, c % 2
        m = dict(shared)
        m["xo"] = np.ascontiguousarray(x[b, s * TOK:(s + 1) * TOK])
        m["xp"] = np.ascontiguousarray(x[b, 0:TOK]) if s == 1 else zeros
        m["c_mask"] = np.full((128, 1), 0.0 if s == 1 else -1.0e9, np.float32)
        kp = kpos.copy()
        if s == 0:
            kp[4, 0:TOK] = -1.0e9
        m["c_kpos"] = kp
        in_maps.append(m)
    return in_maps


def kernel(**inputs):
    in_maps = _prep(inputs)
    nc, _ = build(debug=False)
    res = run_bass_kernel_spmd(nc, in_maps, core_ids=list(range(NCORES)))
    out = np.empty((4, 2 * TOK, D), np.float32)
    for c in range(NCORES):
        b, s = c // 2, c % 2
        out[b, s * TOK:(s + 1) * TOK] = res.results[c]["y"]
    return out
```
